# Optimizing a Trainium2 kernel written in Bass

```python
import math
import jax, jax.numpy as jnp
from jax import lax
import numpy as np

D_MODEL = 2048
BATCH = 2
SEQ = 8192
DEPTH = 4
DEC_BATCH = 8
DEC_SEQ = 32
PAST_LEN = 4096

CHUNK = 64
D_MIX = D_MODEL
LRU_WIDTH = D_MIX // 4
LRU_BLOCKS = 4
LRU_BLOCK = LRU_WIDTH // LRU_BLOCKS
CONV_W = 4
LRU_C = 8.0
ATT_WIDTH = D_MIX // 2
N_HEADS = 8
HEAD_V = ATT_WIDTH // N_HEADS
HEAD_QK = HEAD_V // 2
ROT_DIM = HEAD_QK // 4
ROPE_THETA = 500000.0
MLP_WIDTH = D_MIX - LRU_WIDTH - ATT_WIDTH
MLP_GROUPS = 4
MLP_GROUP = MLP_WIDTH // MLP_GROUPS
MLP_CHUNK = 128
D_FF = ((8 * D_MODEL // 3 + 255) // 256) * 256
Q_BLOCK = 128
EPS = 1e-6
IN_COLS = 2 * LRU_WIDTH + 3 * ATT_WIDTH + 2 * MLP_WIDTH
SPLITS = (LRU_WIDTH, 2 * LRU_WIDTH, 2 * LRU_WIDTH + ATT_WIDTH, 2 * LRU_WIDTH + 2 * ATT_WIDTH,
          2 * LRU_WIDTH + 3 * ATT_WIDTH, 2 * LRU_WIDTH + 3 * ATT_WIDTH + MLP_WIDTH)

kernel_name = 'hybrid_streaming_rglru_diffattn_chunkmlp_step'


def rmsnorm(x, g):
    xf = x.astype(jnp.float32)
    y = xf * lax.rsqrt(jnp.mean(xf * xf, axis=-1, keepdims=True) + EPS)
    return (y * g.astype(jnp.float32)).astype(x.dtype)


def layernorm(x, g, b):
    xf = x.astype(jnp.float32)
    xc = xf - jnp.mean(xf, axis=-1, keepdims=True)
    y = xc * lax.rsqrt(jnp.mean(xc * xc, axis=-1, keepdims=True) + EPS)
    return (y * g.astype(jnp.float32) + b.astype(jnp.float32)).astype(x.dtype)


def apply_rope(x, pos):
    half = ROT_DIM // 2
    inv_freq = jnp.power(jnp.float32(ROPE_THETA), -jnp.arange(half, dtype=jnp.float32) * (2.0 / ROT_DIM))
    ang = pos.astype(jnp.float32)[:, None] * inv_freq[None, :]
    cos = jnp.cos(ang)[:, None, None, :]
    sin = jnp.sin(ang)[:, None, None, :]
    xr = x[..., :ROT_DIM].astype(jnp.float32)
    x1, x2 = xr[..., :half], xr[..., half:]
    rot = jnp.concatenate([x1 * cos - x2 * sin, x2 * cos + x1 * sin], axis=-1).astype(x.dtype)
    return jnp.concatenate([rot, x[..., ROT_DIM:]], axis=-1)


def causal_conv(x_ext, w, b):
    L = x_ext.shape[1] - (CONV_W - 1)
    out = b
    for j in range(CONV_W):
        out = out + x_ext[:, j:j + L] * w[j]
    return out


def block_diag(x, w, b):
    B, L, _ = x.shape
    y = jnp.einsum('blhc,hcd->blhd', x.reshape(B, L, LRU_BLOCKS, LRU_BLOCK), w)
    return y.reshape(B, L, LRU_WIDTH) + b


def _lin_combine(left, right):
    a1, b1 = left
    a2, b2 = right
    return a1 * a2, a2 * b1 + b2


def rg_lru(x, h0, w_a, b_a, w_x, b_x, lam):
    r = jax.nn.sigmoid(block_diag(x, w_a, b_a)).astype(jnp.float32)
    i = jax.nn.sigmoid(block_diag(x, w_x, b_x)).astype(jnp.float32)
    log_a = -LRU_C * r * jax.nn.softplus(-lam.astype(jnp.float32))
    a = jnp.exp(log_a)
    bt = jnp.sqrt(-jnp.expm1(2.0 * log_a)) * i * x.astype(jnp.float32)
    a_cum, b_cum = lax.associative_scan(_lin_combine, (a, bt), axis=1)
    h = b_cum + a_cum * h0.astype(jnp.float32)[:, None, :]
    return h.astype(x.dtype), h[:, -1].astype(x.dtype)


def diff_attention(q, k, v, lam, mask):
    s = jnp.einsum('bqhme,bkhme->bhmqk', q, k).astype(jnp.float32) * (HEAD_QK ** -0.5)
    s = jnp.where(mask, s, -jnp.inf)
    p = jax.nn.softmax(s, axis=-1)
    p = p[:, :, 0] - lam * p[:, :, 1]
    return jnp.einsum('bhqk,bkhd->bqhd', p.astype(v.dtype), v)


def prompt_attention(q, k, v, lam):
    B, L = q.shape[0], q.shape[1]
    nb = L // Q_BLOCK
    qb = jnp.moveaxis(q.reshape(B, nb, Q_BLOCK, N_HEADS, 2, HEAD_QK), 1, 0)
    k_chunk = jnp.arange(L) // CHUNK

    def one_block(args):
        q_blk, start = args
        q_chunk = (start + jnp.arange(Q_BLOCK)) // CHUNK
        mask = k_chunk[None, :] <= q_chunk[:, None]
        return diff_attention(q_blk, k, v, lam, mask)

    o = lax.map(one_block, (qb, jnp.arange(nb) * Q_BLOCK))
    return jnp.moveaxis(o, 0, 1).reshape(B, L, N_HEADS, HEAD_V)


def chunk_mlp(u, v, w_s, b_s):
    B, L, _ = v.shape
    n = max(L // MLP_CHUNK, 1)
    P = min(L, MLP_CHUNK)
    w = jnp.tril(w_s[:, :P, :P])
    b = b_s[:, :P]
    vc = v.reshape(B, n, P, MLP_GROUPS, MLP_GROUP)
    s = jnp.einsum('gpq,bnqgc->bnpgc', w, vc) + b.T[None, None, :, :, None]
    return u * s.reshape(B, L, MLP_WIDTH)


def layer(x, pos, k_past, v_past, h0, conv_buf, p, lam_init):
    B, L, _ = x.shape
    xn = rmsnorm(x, p['g_mix_pre'])
    z = jnp.einsum('bld,dc->blc', xn, p['w_in'])
    xa, ga, q, k, v, u_c, v_c = jnp.split(z, SPLITS, axis=-1)

    xa_ext = jnp.concatenate([conv_buf.astype(xa.dtype), xa], axis=1)
    xc = causal_conv(xa_ext, p['conv_w'], p['conv_b'])
    h, h_last = rg_lru(xc, h0, p['w_rg_a'], p['b_rg_a'], p['w_rg_x'], p['b_rg_x'], p['lru_lambda'])
    y_a = h * jax.nn.gelu(ga)
    new_conv = xa_ext[:, -(CONV_W - 1):]

    q = apply_rope(q.reshape(B, L, N_HEADS, 2, HEAD_QK), pos)
    k = apply_rope(k.reshape(B, L, N_HEADS, 2, HEAD_QK), pos)
    v = v.reshape(B, L, N_HEADS, HEAD_V)
    f32 = jnp.float32
    lam = (jnp.exp(jnp.sum(p['lam_q1'].astype(f32) * p['lam_k1'].astype(f32)))
           - jnp.exp(jnp.sum(p['lam_q2'].astype(f32) * p['lam_k2'].astype(f32))) + lam_init)
    if k_past is None:
        o = prompt_attention(q, k, v, lam)
    else:
        k_all = jnp.concatenate([k_past.reshape(B, -1, N_HEADS, 2, HEAD_QK).astype(k.dtype), k], axis=1)
        v_all = jnp.concatenate([v_past.astype(v.dtype), v], axis=1)
        mask = (jnp.arange(k_all.shape[1]) // CHUNK)[None, :] <= (pos // CHUNK)[:, None]
        o = diff_attention(q, k_all, v_all, lam, mask)
    y_b = (rmsnorm(o, p['g_subln']) * (1.0 - lam_init)).reshape(B, L, ATT_WIDTH)

    u_c = jax.nn.gelu(u_c)
    v_c = layernorm(jax.nn.gelu(v_c), p['g_mlp_v'], p['b_mlp_v'])
    y_c = chunk_mlp(u_c, v_c, p['w_spatial'], p['b_spatial'])

    y = jnp.einsum('blc,cd->bld', jnp.concatenate([y_a, y_b, y_c], axis=-1), p['w_out'])
    x = x + rmsnorm(y, p['g_mix_post'])

    hn = rmsnorm(x, p['g_ffn_pre'])
    ff = jax.nn.silu(jnp.einsum('bld,df->blf', hn, p['w_gate'])) * jnp.einsum('bld,df->blf', hn, p['w_up'])
    x = x + rmsnorm(jnp.einsum('blf,fd->bld', ff, p['w_down']), p['g_ffn_post'])
    return x, k.reshape(B, L, N_HEADS, 2 * HEAD_QK), v, h_last, new_conv, v_c


def setup_inputs(seed: int = 0) -> dict:
    key = jax.random.key(seed)
    ks = iter(jax.random.split(key, 40))

    def nrm(shape, scale):
        return jax.random.normal(next(ks), shape, jnp.float32) * scale

    def gain(shape):
        return 1.0 + nrm(shape, 0.02)

    u = jax.random.uniform(next(ks), (DEPTH, LRU_WIDTH), jnp.float32, minval=0.9, maxval=0.999)
    a_base = u ** (1.0 / LRU_C)
    lru_lambda = jnp.log(a_base) - jnp.log1p(-a_base)
    return {
        'x_prompt': nrm((BATCH, SEQ, D_MODEL), 1.0),
        'x_sample': nrm((DEC_BATCH, DEC_SEQ, D_MODEL), 1.0),
        'cache_k': nrm((DEPTH, DEC_BATCH, PAST_LEN, N_HEADS, 2 * HEAD_QK), 1.0),
        'cache_v': nrm((DEPTH, DEC_BATCH, PAST_LEN, N_HEADS, HEAD_V), 1.0),
        'state_lru_h': nrm((DEPTH, DEC_BATCH, LRU_WIDTH), 0.5),
        'state_conv': nrm((DEPTH, DEC_BATCH, CONV_W - 1, LRU_WIDTH), 1.0),
        'g_mix_pre': gain((DEPTH, D_MODEL)),
        'w_in': nrm((DEPTH, D_MODEL, IN_COLS), D_MODEL ** -0.5),
        'conv_w': nrm((DEPTH, CONV_W, LRU_WIDTH), CONV_W ** -0.5),
        'conv_b': nrm((DEPTH, LRU_WIDTH), 0.01),
        'w_rg_a': nrm((DEPTH, LRU_BLOCKS, LRU_BLOCK, LRU_BLOCK), LRU_BLOCK ** -0.5),
        'b_rg_a': nrm((DEPTH, LRU_WIDTH), 0.01),
        'w_rg_x': nrm((DEPTH, LRU_BLOCKS, LRU_BLOCK, LRU_BLOCK), LRU_BLOCK ** -0.5),
        'b_rg_x': nrm((DEPTH, LRU_WIDTH), 0.01),
        'lru_lambda': lru_lambda,
        'lam_q1': nrm((DEPTH, HEAD_QK), 0.1),
        'lam_k1': nrm((DEPTH, HEAD_QK), 0.1),
        'lam_q2': nrm((DEPTH, HEAD_QK), 0.1),
        'lam_k2': nrm((DEPTH, HEAD_QK), 0.1),
        'g_subln': gain((DEPTH, HEAD_V)),
        'g_mlp_v': gain((DEPTH, MLP_WIDTH)),
        'b_mlp_v': nrm((DEPTH, MLP_WIDTH), 0.01),
        'w_spatial': nrm((DEPTH, MLP_GROUPS, MLP_CHUNK, MLP_CHUNK), MLP_CHUNK ** -0.5),
        'b_spatial': 1.0 + nrm((DEPTH, MLP_GROUPS, MLP_CHUNK), 0.01),
        'w_out': nrm((DEPTH, D_MIX, D_MODEL), D_MIX ** -0.5),
        'g_mix_post': gain((DEPTH, D_MODEL)),
        'g_ffn_pre': gain((DEPTH, D_MODEL)),
        'w_gate': nrm((DEPTH, D_MODEL, D_FF), D_MODEL ** -0.5),
        'w_up': nrm((DEPTH, D_MODEL, D_FF), D_MODEL ** -0.5),
        'w_down': nrm((DEPTH, D_FF, D_MODEL), D_FF ** -0.5),
        'g_ffn_post': gain((DEPTH, D_MODEL)),
    }


def reference(x_prompt, x_sample, cache_k, cache_v, state_lru_h, state_conv,
              g_mix_pre, w_in, conv_w, conv_b, w_rg_a, b_rg_a, w_rg_x, b_rg_x, lru_lambda,
              lam_q1, lam_k1, lam_q2, lam_k2, g_subln, g_mlp_v, b_mlp_v, w_spatial, b_spatial,
              w_out, g_mix_post, g_ffn_pre, w_gate, w_up, w_down, g_ffn_post):
    pos_p = jnp.arange(x_prompt.shape[1])
    pos_s = PAST_LEN + jnp.arange(x_sample.shape[1])
    bp = x_prompt.shape[0]
    xp, xs = x_prompt, x_sample
    kps, vps, hps, cps = [], [], [], []
    kss, vss, hss, css, vcs = [], [], [], [], []
    for l in range(DEPTH):
        p = {
            'g_mix_pre': g_mix_pre[l], 'w_in': w_in[l], 'conv_w': conv_w[l], 'conv_b': conv_b[l],
            'w_rg_a': w_rg_a[l], 'b_rg_a': b_rg_a[l], 'w_rg_x': w_rg_x[l], 'b_rg_x': b_rg_x[l],
            'lru_lambda': lru_lambda[l], 'lam_q1': lam_q1[l], 'lam_k1': lam_k1[l],
            'lam_q2': lam_q2[l], 'lam_k2': lam_k2[l], 'g_subln': g_subln[l],
            'g_mlp_v': g_mlp_v[l], 'b_mlp_v': b_mlp_v[l], 'w_spatial': w_spatial[l],
            'b_spatial': b_spatial[l], 'w_out': w_out[l], 'g_mix_post': g_mix_post[l],
            'g_ffn_pre': g_ffn_pre[l], 'w_gate': w_gate[l], 'w_up': w_up[l], 'w_down': w_down[l],
            'g_ffn_post': g_ffn_post[l],
        }
        lam_init = 0.8 - 0.6 * math.exp(-0.3 * l)
        h0 = jnp.zeros((bp, LRU_WIDTH), xp.dtype)
        cb0 = jnp.zeros((bp, CONV_W - 1, LRU_WIDTH), xp.dtype)
        xp, k_p, v_p, h_p, c_p, _ = layer(xp, pos_p, None, None, h0, cb0, p, lam_init)
        kps.append(k_p); vps.append(v_p); hps.append(h_p); cps.append(c_p)
        xs, k_s, v_s, h_s, c_s, vc_s = layer(xs, pos_s, cache_k[l], cache_v[l], state_lru_h[l],
                                             state_conv[l], p, lam_init)
        kss.append(k_s); vss.append(v_s); hss.append(h_s); css.append(c_s); vcs.append(vc_s)
    return (xp, xs, jnp.stack(kps), jnp.stack(vps), jnp.stack(hps), jnp.stack(cps),
            jnp.stack(kss), jnp.stack(vss), jnp.stack(hss), jnp.stack(css), jnp.stack(vcs))
```

```python
import math
import types
import numpy as np
import ml_dtypes
import concourse.bass as bass
import concourse.mybir as mybir
from concourse.bass_utils import run_bass_kernel_spmd

F32 = mybir.dt.float32
BF16 = mybir.dt.bfloat16
U8 = mybir.dt.uint8
ALU = mybir.AluOpType
AF = mybir.ActivationFunctionType

D = 2048
DT = 16
NH = 8
TS = 32
EPS = 1e-6
NEG = -30000.0


class Cfg:
    def __init__(self, L=4, TP=2048, PAST=4096, DFF=5632):
        self.L, self.TP, self.PAST, self.DFF = L, TP, PAST, DFF
        self.NPT = TP // 128
        self.NT = self.NPT + 1
        self.NTOK = TP + TS
        self.FT = DFF // 128
        self.NQB = TP // 512
        self.NKT = TP // 128
        self.PKT = PAST // 128


class Op:
    __slots__ = ("eng", "fn", "deps", "dma", "signal", "sigval", "sem", "target", "prev_target", "inc")


class Sched:
    ENGS = ("pe", "act", "dve", "pool", "sp")

    def __init__(self, n_dma_sems=40):
        self.ops = []
        self.last_w = {}
        self.readers = {}
        self.n_dma_sems = n_dma_sems
        self.dma_rr = 0
        self.dma_sem_val = [0] * n_dma_sems
        self.fence_op = None

    @staticmethod
    def _freeze(fn):
        if fn.__closure__ is None:
            return fn
        cells = []
        for c in fn.__closure__:
            try:
                cells.append(types.CellType(c.cell_contents))
            except ValueError:
                cells.append(c)
        return types.FunctionType(fn.__code__, fn.__globals__, fn.__name__, fn.__defaults__, tuple(cells))

    def op(self, eng, fn, reads=(), writes=(), dma=False, inc=16):
        fn = self._freeze(fn)
        o = Op()
        o.eng, o.fn, o.dma, o.signal, o.sigval, o.inc = eng, fn, dma, False, 0, inc
        idx = len(self.ops)
        deps = {}
        for k in reads:
            w = self.last_w.get(k)
            if w is not None:
                deps[w] = True
        for k in writes:
            w = self.last_w.get(k)
            if w is not None:
                deps[w] = True
            for r in self.readers.get(k, ()):
                if r not in deps:
                    deps[r] = False
        if self.fence_op is not None:
            deps[self.fence_op] = True
        o.deps = deps
        if dma:
            s = self.dma_rr % self.n_dma_sems
            self.dma_rr += 1
            o.sem = s
            o.prev_target = self.dma_sem_val[s]
            self.dma_sem_val[s] += inc
            o.target = self.dma_sem_val[s]
        self.ops.append(o)
        for k in reads:
            self.readers.setdefault(k, []).append(idx)
        for k in writes:
            self.last_w[k] = idx
            self.readers[k] = []
        for k in reads:
            lst = self.readers[k]
            if len(lst) > 12:
                keep = {}
                out = []
                for r in lst:
                    ro = self.ops[r]
                    if ro.dma:
                        out.append(r)
                    else:
                        keep[ro.eng] = r
                self.readers[k] = out + list(keep.values())
        return idx

    def emit(self, nc):
        ops = self.ops
        for o in ops:
            for d, hard in o.deps.items():
                p = ops[d]
                if p.dma:
                    continue
                if p.eng == o.eng and not o.dma:
                    if p.eng == "pe" or not hard:
                        continue
                p.signal = True
        cnt = {e: 0 for e in self.ENGS}
        for o in ops:
            if not o.dma and o.signal:
                cnt[o.eng] += 1
                o.sigval = cnt[o.eng]
        engobj = {"pe": nc.tensor, "act": nc.scalar, "dve": nc.vector, "pool": nc.gpsimd, "sp": nc.sync}
        import contextlib
        with contextlib.ExitStack() as st:
            esem = {e: st.enter_context(nc.semaphore("eng_" + e)) for e in self.ENGS}
            dsem = [st.enter_context(nc.semaphore("dma_%d" % i)) for i in range(self.n_dma_sems)]
            block = st.enter_context(nc.Block())

            def run(ename):
                def body(eng):
                    waited_e = {e: 0 for e in self.ENGS}
                    waited_d = {}
                    for o in ops:
                        if o.eng != ename:
                            continue
                        need_e = {}
                        need_d = {}
                        for d, hard in o.deps.items():
                            p = ops[d]
                            if p.dma:
                                if need_d.get(p.sem, 0) < p.target:
                                    need_d[p.sem] = p.target
                            else:
                                if p.eng == o.eng and not o.dma:
                                    if p.eng == "pe" or not hard:
                                        continue
                                if need_e.get(p.eng, 0) < p.sigval:
                                    need_e[p.eng] = p.sigval
                        if o.dma and o.prev_target > 0:
                            if need_d.get(o.sem, 0) < o.prev_target:
                                need_d[o.sem] = o.prev_target
                        for e, v in need_e.items():
                            if waited_e[e] < v:
                                eng.wait_ge(esem[e], v)
                                waited_e[e] = v
                        for s, v in need_d.items():
                            if waited_d.get(s, 0) < v:
                                eng.wait_ge(dsem[s], v)
                                waited_d[s] = v
                        ins = o.fn(eng)
                        if o.dma:
                            ins.then_inc(dsem[o.sem], o.inc)
                        elif o.signal:
                            ins.then_inc(esem[o.eng], 1)
                    if ename in ("sp", "pool"):
                        last = {}
                        for o in ops:
                            if o.dma and o.eng == ename:
                                last[o.sem] = max(last.get(o.sem, 0), o.target)
                        for s, v in last.items():
                            if waited_d.get(s, 0) < v:
                                eng.wait_ge(dsem[s], v)
                return body

            block.sync(run("sp"))
            block.gpsimd(run("pool"))
            block.scalar(run("act"))
            block.vector(run("dve"))
            block.tensor(run("pe"))


class Builder:
    def __init__(self, cfg):
        self.cfg = cfg
        self.nc = bass.Bass("TRN2", target_bir_lowering=False)
        self.S = Sched()
        self.arena_off = 0
        self.dram = {}

    def din(self, name, shape, dt=F32):
        t = self.nc.dram_tensor(name, list(shape), dt, kind="ExternalInput")
        self.dram[name] = t
        return t.ap()

    def dout(self, name, shape, dt=F32):
        t = self.nc.dram_tensor(name, list(shape), dt, kind="ExternalOutput")
        self.dram[name] = t
        return t.ap()

    def dscr(self, name, shape, dt):
        if getattr(self.cfg, "debug", False) and name in ("xres_p", "xres_s", "yscr", "qt_d", "xa_d", "lru_a", "lru_b", "dbg_ycat"):
            t = self.nc.dram_tensor(name, list(shape), dt, kind="ExternalOutput")
        else:
            t = self.nc.dram_tensor(name, list(shape), dt)
        self.dram[name] = t
        return t.ap()

    def salloc(self, shape, dt, off=None):
        sz = 4 if dt == F32 else 2
        n = int(np.prod(shape[1:]))
        nbytes = (n * sz + 63) // 64 * 64
        if off is None:
            off = self.arena_off
            self.arena_off += nbytes
        v = self.A[:, off:off + n * sz].bitcast(dt)
        if len(shape) == 3:
            v = v.rearrange("p (a b) -> p a b", a=shape[1])
        elif len(shape) == 4:
            v = v.rearrange("p (a b c) -> p a b c", a=shape[1], b=shape[2])
        return v

    def act(self, fn, r=(), w=()):
        return self.S.op("act", fn, r, w)

    def dve(self, fn, r=(), w=()):
        return self.S.op("dve", fn, r, w)

    def pool(self, fn, r=(), w=()):
        return self.S.op("pool", fn, r, w)

    def pe(self, fn, r=(), w=()):
        return self.S.op("pe", fn, r, w)

    def dma(self, q, out, in_, r=(), w=()):
        return self.S.op(q, lambda e: e.dma_start(out=out, in_=in_), r, w, dma=True, inc=16)

    def build(self):
        cfg, nc = self.cfg, self.nc
        L, TP, PAST, DFF, NT, NPT, NTOK, FT = cfg.L, cfg.TP, cfg.PAST, cfg.DFF, cfg.NT, cfg.NPT, cfg.NTOK, cfg.FT
        NCH = TP // 512
        NKT = TP // 128
        PKT = PAST // 128
        FQ = FT // 4
        S = self.S
        X = mybir.AxisListType.X
        xp = self.din("xp", [TP, D])
        xs = self.din("xs", [TS, D])
        ck = self.din("ck", [L, PAST, 1024])
        cv = self.din("cv", [L, PAST, 1024])
        w_in = self.din("w_in", [L, D, 5120])
        w_out = self.din("w_out", [L, D, D])
        w_gate = self.din("w_gate", [L, D, DFF])
        w_up = self.din("w_up", [L, D, DFF])
        w_down = self.din("w_down", [L, DFF, D])
        w_rg = self.din("w_rg", [L, 8, 128, 128])
        w_sp = self.din("w_sp", [L, 4, 128, 128])
        NPV = 81
        pvec = self.din("pvec", [L, 128, NPV])
        NBV = 5888
        bvec = self.din("bvec", [L, NBV])
        cosd = self.din("cosd", [128, NT, 64])
        sind = self.din("sind", [128, NT, 64])
        NCC = 12
        cconst = self.din("cconst", [128, NCC])
        masks = self.din("masks", [128, 4, 512])
        triu = self.din("triu", [128, 128])
        ident = self.din("ident", [128, 128])

        y_p = self.dout("y_p", [TP, D])
        y_s = self.dout("y_s", [TS, D])
        k_p = self.dout("k_p", [L, TP, 1024])
        v_p = self.dout("v_p", [L, TP, 1024])
        h_p = self.dout("h_p", [L, 128, 4])
        c_p = self.dout("c_p", [L, 128, 4, 3])
        k_s = self.dout("k_s", [L, TS, 1024])
        v_s = self.dout("v_s", [L, TS, 1024])
        h_s = self.dout("h_s", [L, 128, 4])
        c_s = self.dout("c_s", [L, 128, 4, 3])
        cv_s = self.dout("cv_s", [L, TS, 512])

        xres_p = self.dscr("xres_p", [TP, D], F32)
        xres_s = self.dscr("xres_s", [TS, D], F32)
        yscr = self.dscr("yscr", [TP + TS, D], F32)
        KT_own = [self.dscr("kt_own%d" % i, [256, TP], BF16) for i in range(4)]
        KT_all = [self.dscr("kt_all%d" % i, [4 * 256, TP], BF16) for i in range(4)]
        NVC = TP // 512
        V_own = [self.dscr("v_own%d" % i, [512, 1024], BF16) for i in range(NVC)]
        V_all = [self.dscr("v_all%d" % i, [4 * 512, 1024], BF16) for i in range(NVC)]
        QT_d = self.dscr("qt_d", [8, 128, NTOK], BF16)
        halo_in = self.dscr("halo_in", [128, 12], F32)
        halo_all = self.dscr("halo_all", [4 * 128, 12], F32)
        ab_in = self.dscr("ab_in", [128, 8], F32)
        ab_all = self.dscr("ab_all", [4 * 128, 8], F32)
        xa_d = self.dscr("xa_d", [4, 128, TP], F32)
        xa_sd = self.dscr("xa_sd", [4, 128, TS], F32)
        lru_a = self.dscr("lru_a", [4, 128, TP], F32)
        lru_b = self.dscr("lru_b", [4, 128, TP], F32)
        dbg_ycat = self.dscr("dbg_ycat", [128, 16, NTOK], BF16) if getattr(cfg, "debug", False) else None

        ARENA = 206 * 1024
        self.arena_t = nc.alloc_sbuf_tensor("arena", [128, ARENA], U8)
        self.A = self.arena_t.ap()
        sa = self.salloc
        identF = sa([128, 128], F32)
        identB = sa([128, 128], BF16)
        onesB = sa([128, 128], BF16)
        onesF = sa([128, 128], F32)
        triuB = sa([128, 128], BF16)
        maskB = sa([128, 4, 512], BF16)
        cosT = sa([128, NT, 64], F32)
        sinT = sa([128, NT, 64], F32)
        cc = sa([128, NCC], F32)
        pv = sa([128, NPV + 3], F32)
        pv2 = sa([128, 32], F32)
        lamw = sa([128, 256], F32)
        wrg = sa([128, 8, 128], BF16)
        wspT = sa([128, 4, 128], BF16)
        wsp_raw = sa([128, 4, 128], BF16)
        small = sa([128, 96], F32)
        ksT = sa([128, 8, TS], BF16)
        vsB = sa([128, 1024], BF16)
        base_nc = self.arena_off
        ycat = sa([128, 16, NTOK], BF16)
        base_ph = self.arena_off
        xnT = sa([128, 16, 544], BF16)
        wbuf = [sa([128, 16, 512], BF16) for _ in range(2)]
        xt = [sa([128, 2048], F32) for _ in range(2)]
        xnb = sa([128, 2048], BF16)
        zf = [sa([128, 512], F32) for _ in range(2)]
        rt = sa([128, 4, 64], F32)
        stg = [sa([128, 512], BF16) for _ in range(2)]
        gel = [sa([128, 512], F32) for _ in range(3)]
        uact = sa([128, 4, 544], BF16)
        vnb = sa([128, 5, 512], BF16)
        mlpv = sa([128, 1536], F32)
        end_A = self.arena_off
        assert end_A <= ARENA, ("phase A overflow", end_A)
        self.arena_off = base_ph
        xa = sa([128, 3 + TP + 3 + TS], F32)
        xc = sa([128, TP + TS], F32)
        xcb = sa([128, TP + TS], BF16)
        la = [sa([128, TP], F32) for _ in range(2)]
        las = sa([128, 4, 2, TS], F32)
        hbuf = sa([128, TP], F32)
        lr = [sa([128, 512], F32) for _ in range(6)]
        halo_sb = sa([128, 4, 12], F32)
        ab_sb = sa([128, 4, 8], F32)
        ab_st = sa([128, 8], F32)
        assert self.arena_off <= ARENA, ("lru overflow", self.arena_off)
        self.arena_off = base_ph
        KTh = [sa([128, 4 * TP], BF16) for _ in range(2)]
        Vh = [sa([128, 4 * NKT, 128], BF16) for _ in range(2)]
        QTh = [sa([128, NTOK], BF16) for _ in range(2)]
        Pb = [sa([128, 512], BF16) for _ in range(4)]
        fin = [sa([128, 512], F32) for _ in range(6)]
        kc = sa([128, PKT, 128], BF16)
        vc = sa([128, PKT + 1, 128], BF16)
        KTc = sa([128, PAST + TS], BF16)
        assert self.arena_off <= ARENA, ("attention overflow", self.arena_off)
        self.arena_off = base_ph
        wout_sb = sa([128, 16, 2048], BF16)
        wxt = [sa([128, 2048], F32) for _ in range(2)]
        gpostA = sa([128, 2048], F32)
        wtmp = [sa([128, 512], F32) for _ in range(4)]
        wjunk = sa([128, 512], BF16)
        assert self.arena_off <= ARENA, ("wout overflow", self.arena_off)
        self.arena_off = base_nc
        hnT = sa([128, 16, 544], BF16)
        ffT = sa([128, FT, 544], BF16)
        gub = [sa([128, 16, 256], BF16) for _ in range(4)]
        wdb = [sa([128, FQ, 512], BF16) for _ in range(2)]
        fxt = [sa([128, 2048], F32) for _ in range(2)]
        fyt = [sa([128, 2048], F32) for _ in range(2)]
        fxnb = sa([128, 2048], BF16)
        gpostF = sa([128, 2048], F32)
        ft_ = [sa([128, 512], F32) for _ in range(4)]
        ystg = [sa([128, 512], F32) for _ in range(2)]
        fjunk = sa([128, 512], BF16)
        assert self.arena_off <= ARENA, ("ffn overflow", self.arena_off)

        PS = [nc.alloc_psum_tensor("ps%d" % i, [128, 512], F32).ap() for i in range(8)]

        def psB(i):
            return PS[i].bitcast(BF16)

        self.fence_keys = {}

        def K(*a):
            self.fence_keys[a] = True
            return a

        def fence():
            keys = list(self.fence_keys.keys())
            self.S.fence_op = self.dve(lambda e: e.memset(small[:, 90:91], 0.0), r=keys, w=keys)

        self.dma("sp", identF, ident[:, :], w=[K("identF")])
        self.dma("pool", identB, ident[:, :], w=[K("identB")])
        self.dma("pool", triuB, triu[:, :], w=[K("triuB")])
        self.dma("pool", maskB, masks[:, :, :], w=[K("maskB")])
        self.dma("sp", cosT, cosd[:, :, :], w=[K("cosT")])
        self.dma("sp", sinT, sind[:, :, :], w=[K("sinT")])
        self.dma("sp", cc, cconst[:, :], w=[K("cc")])
        self.dve(lambda e: e.memset(onesB, 1.0), w=[K("onesB")])
        self.dve(lambda e: e.memset(onesF, 1.0), w=[K("onesF")])

        def tokn(tt):
            return 128 if tt < NPT else TS

        def tcol(tt):
            return 128 * tt

        def x_src(l, tt):
            if l == 0:
                return xp[128 * tt:128 * tt + 128, :] if tt < NPT else xs[:, :]
            return xres_p[128 * tt:128 * tt + 128, :] if tt < NPT else xres_s[:, :]

        def xres_ap(tt):
            return xres_p[128 * tt:128 * tt + 128, :] if tt < NPT else xres_s[:, :]

        def yout_ap(tt):
            return y_p[128 * tt:128 * tt + 128, :] if tt < NPT else y_s[:, :]

        def xkey(tt):
            return K("xres", tt)

        def chunk_tiles(ci):
            tl = [(4 * ci + i, 128 * i, 128) for i in range(4)]
            if ci == NCH - 1:
                tl.append((NPT, 512, TS))
            return tl

        def chunk_subs(ci):
            return [(0, 512)] + ([(512, TS)] if ci == NCH - 1 else [])

        def rstd_from_ss(ss_ap, out_ap, n, rk, wk):
            self.act(lambda e: e.activation(out=out_ap, in_=ss_ap, func=AF.Ln, scale=1.0 / n, bias=EPS), r=rk, w=wk)
            self.act(lambda e: e.activation(out=out_ap, in_=out_ap, func=AF.Exp, scale=-0.5), r=wk, w=wk)

        self.wb_n = 0

        def load_wblock(src_ap):
            s = self.wb_n % 2
            self.wb_n += 1
            key = K("wbuf", s)
            self.dma("pool", wbuf[s], src_ap.rearrange("(kt p) c -> p kt c", p=128), w=[key])
            return wbuf[s], key

        def norm_transpose(n, xtile, xk, dstT, dcol, dkey, g0, xnb_t, kpre):
            ssk = K(kpre + "ss")
            xnk = K(kpre + "xnb")
            self.act(lambda e: e.activation(out=xnb_t[0:n, :], in_=xtile[0:n, :], func=AF.Square,
                                            accum_out=small[0:n, 0:1]), r=[xk], w=[xnk, ssk])
            rstd_from_ss(small[0:n, 0:1], small[0:n, 1:2], D, [ssk], [K(kpre + "rstd")])
            self.dve(lambda e: e.tensor_scalar(out=xnb_t[0:n, :], in0=xtile[0:n, :], scalar1=small[0:n, 1:2],
                                               scalar2=None, op0=ALU.mult),
                     r=[xk, K(kpre + "rstd")], w=[xnk])
            for g4 in range(4):
                bank = 4 + g4
                bk = K("ps", bank)
                for i in range(4):
                    dt_ = g4 * 4 + i
                    self.pe(lambda e, dt_=dt_, i=i, bank=bank: e.transpose(
                        out=psB(bank)[:, i * 128:i * 128 + n], in_=xnb_t[0:n, dt_ * 128:(dt_ + 1) * 128],
                        identity=identB[0:n, 0:n]), r=[xnk, K("identB")], w=[bk])
                for i in range(4):
                    dt_ = g4 * 4 + i
                    if i % 2 == 0:
                        self.act(lambda e, dt_=dt_, i=i, bank=bank: e.activation(
                            out=dstT[:, dt_, dcol:dcol + n], in_=psB(bank)[:, i * 128:i * 128 + n], func=AF.Copy,
                            scale=pv[:, g0 + dt_:g0 + dt_ + 1]), r=[bk, K("pv")], w=[dkey])
                    else:
                        self.dve(lambda e, dt_=dt_, i=i, bank=bank: e.tensor_scalar(
                            out=dstT[:, dt_, dcol:dcol + n], in0=psB(bank)[:, i * 128:i * 128 + n],
                            scalar1=pv[:, g0 + dt_:g0 + dt_ + 1], scalar2=None, op0=ALU.mult),
                            r=[bk, K("pv")], w=[dkey])

        def gelu(src, dst, n, ncol, rk, wk):
            t0, t1 = gel[0], gel[1]
            k0, k1 = K("gel", 0), K("gel", 1)
            self.act(lambda e: e.activation(out=t0[0:n, 0:ncol], in_=src, func=AF.Square), r=rk, w=[k0])
            self.dve(lambda e: e.tensor_scalar(out=t0[0:n, 0:ncol], in0=t0[0:n, 0:ncol], scalar1=0.044715,
                                               scalar2=1.0, op0=ALU.mult, op1=ALU.add), r=[k0], w=[k0])
            self.dve(lambda e: e.tensor_tensor(out=t0[0:n, 0:ncol], in0=src, in1=t0[0:n, 0:ncol], op=ALU.mult),
                     r=rk + [k0], w=[k0])
            self.act(lambda e: e.activation(out=t1[0:n, 0:ncol], in_=t0[0:n, 0:ncol], func=AF.Exp,
                                            scale=-1.5957691216), r=[k0], w=[k1])
            self.dve(lambda e: e.tensor_scalar(out=t1[0:n, 0:ncol], in0=t1[0:n, 0:ncol], scalar1=1.0, scalar2=None,
                                               op0=ALU.add), r=[k1], w=[k1])
            self.dve(lambda e: e.reciprocal(out=t1[0:n, 0:ncol], in_=t1[0:n, 0:ncol]), r=[k1], w=[k1])
            self.dve(lambda e: e.tensor_tensor(out=dst, in0=src, in1=t1[0:n, 0:ncol], op=ALU.mult),
                     r=rk + [k1], w=wk)

        RG = [[0, 1, 2, 3], [4, 5, 6, 7]]

        def allgather(src_ap, dst_ap, rk, wk):
            S.op("pool", lambda e: e.collective_compute("AllGather", ALU.bypass, replica_groups=RG,
                                                        ins=[src_ap.opt()], outs=[dst_ap.opt()]),
                 rk, wk, dma=True, inc=1)

        bankrr = [0]

        def nb():
            b = bankrr[0] % 4
            bankrr[0] += 1
            return b

        for l in range(L):
            lam_init = 0.8 - 0.6 * math.exp(-0.3 * l)
            final = (l == L - 1)
            self.dma("sp", pv[:, 0:NPV], pvec[l, :, :], w=[K("pv")])
            self.dma("sp", lamw, bvec[l:l + 1, 5632:5888].partition_broadcast(128), w=[K("lamw")])
            self.dma("sp", mlpv, bvec[l:l + 1, 4096:5632].partition_broadcast(128), w=[K("mlpv")])
            self.dma("pool", wrg, w_rg[l].rearrange("a c d -> c a d"), w=[K("wrg")])
            self.dma("pool", wsp_raw, w_sp[l].rearrange("g p q -> p g q"), w=[K("wsp_raw")])
            self.dve(lambda e: e.tensor_scalar(out=pv2[:, 0:8], in0=pv[:, 52:60], scalar1=-1.0, scalar2=None,
                                               op0=ALU.mult), r=[K("pv")], w=[K("pv2a")])
            self.act(lambda e: e.activation(out=pv2[:, 16:20], in_=pv[:, 60:64], func=AF.Exp, scale=-1.0),
                     r=[K("pv")], w=[K("pv2t")])
            self.act(lambda e: e.activation(out=pv2[:, 16:20], in_=pv2[:, 16:20], func=AF.Ln, scale=1.0, bias=1.0),
                     r=[K("pv2t")], w=[K("pv2t")])
            self.dve(lambda e: e.tensor_scalar(out=pv2[:, 8:12], in0=pv2[:, 16:20], scalar1=-8.0, scalar2=None,
                                               op0=ALU.mult), r=[K("pv2t")], w=[K("pv2cl")])
            self.dve(lambda e: e.tensor_scalar(out=pv2[:, 12:13], in0=pv[:, 64:65], scalar1=float(1.0 - lam_init),
                                               scalar2=None, op0=ALU.mult), r=[K("pv")], w=[K("pv2gs")])
            self.dve(lambda e: e.tensor_tensor(out=lamw[:, 0:64], in0=lamw[:, 0:64], in1=lamw[:, 64:128], op=ALU.mult),
                     r=[K("lamw")], w=[K("lamw")])
            self.dve(lambda e: e.tensor_tensor(out=lamw[:, 128:192], in0=lamw[:, 128:192], in1=lamw[:, 192:256],
                                               op=ALU.mult), r=[K("lamw")], w=[K("lamw")])
            self.dve(lambda e: e.tensor_reduce(out=pv2[:, 20:21], in_=lamw[:, 0:64], axis=X, op=ALU.add),
                     r=[K("lamw")], w=[K("pv2l")])
            self.dve(lambda e: e.tensor_reduce(out=pv2[:, 21:22], in_=lamw[:, 128:192], axis=X, op=ALU.add),
                     r=[K("lamw")], w=[K("pv2l")])
            self.act(lambda e: e.activation(out=pv2[:, 20:22], in_=pv2[:, 20:22], func=AF.Exp), r=[K("pv2l")],
                     w=[K("pv2l")])
            self.dve(lambda e: e.scalar_tensor_tensor(out=pv2[:, 13:14], in0=pv2[:, 21:22], scalar=float(-lam_init),
                                                      in1=pv2[:, 20:21], op0=ALU.add, op1=ALU.subtract),
                     r=[K("pv2l")], w=[K("pv2nl")])
            for g in range(4):
                self.pe(lambda e, g=g: e.transpose(out=psB(4)[:, g * 128:(g + 1) * 128], in_=wsp_raw[:, g, :],
                                                   identity=identB), r=[K("wsp_raw"), K("identB")], w=[K("ps", 4)])
            for g in range(4):
                self.dve(lambda e, g=g: e.tensor_tensor(out=wspT[:, g, :], in0=psB(4)[:, g * 128:(g + 1) * 128],
                                                        in1=triuB, op=ALU.mult),
                         r=[K("ps", 4), K("triuB")], w=[K("wspT")])

            def proj_T(wb, wkey, lc, n, bank):
                for dt_ in range(DT):
                    self.pe(lambda e, dt_=dt_: e.matmul(PS[bank][0:n, :], lhsT=xnT[:, dt_, lc:lc + n],
                                                        rhs=wb[:, dt_, :], start=(dt_ == 0), stop=(dt_ == DT - 1)),
                            r=[K("xnT"), wkey], w=[K("ps", bank)])

            def proj_F(wb, wkey, ct, c0, ncol, bank):
                for dt_ in range(DT):
                    self.pe(lambda e, dt_=dt_: e.matmul(PS[bank][:, 0:ncol], lhsT=wb[:, dt_, ct * 128:(ct + 1) * 128],
                                                        rhs=xnT[:, dt_, c0:c0 + ncol], start=(dt_ == 0),
                                                        stop=(dt_ == DT - 1)),
                            r=[K("xnT"), wkey], w=[K("ps", bank)])

            def rope(zt, zk, n, tt):
                v = zt[0:n, :].rearrange("p (g e) -> p g e", g=8)
                x1, x2 = v[:, :, 0:8], v[:, :, 8:16]
                cs = cosT[0:n, tt, :].rearrange("p (g e) -> p g e", g=8)
                sn = sinT[0:n, tt, :].rearrange("p (g e) -> p g e", g=8)
                t = [rt[0:n, i, :].rearrange("p (g e) -> p g e", g=8) for i in range(4)]
                self.dve(lambda e: e.tensor_tensor(out=t[0], in0=x1, in1=cs, op=ALU.mult), r=[zk, K("cosT")],
                         w=[K("rt", 0)])
                self.dve(lambda e: e.tensor_tensor(out=t[1], in0=x2, in1=sn, op=ALU.mult), r=[zk, K("sinT")],
                         w=[K("rt", 1)])
                self.dve(lambda e: e.tensor_tensor(out=t[2], in0=x2, in1=cs, op=ALU.mult), r=[zk, K("cosT")],
                         w=[K("rt", 2)])
                self.dve(lambda e: e.tensor_tensor(out=t[3], in0=x1, in1=sn, op=ALU.mult), r=[zk, K("sinT")],
                         w=[K("rt", 3)])
                self.dve(lambda e: e.tensor_tensor(out=x1, in0=t[0], in1=t[1], op=ALU.subtract),
                         r=[K("rt", 0), K("rt", 1), zk], w=[zk])
                self.dve(lambda e: e.tensor_tensor(out=x2, in0=t[2], in1=t[3], op=ALU.add),
                         r=[K("rt", 2), K("rt", 3), zk], w=[zk])

            zi = [0]
            for ci in range(NCH):
                tiles = chunk_tiles(ci)
                subs = chunk_subs(ci)
                gc0 = 512 * ci
                for (tt, lc, n) in tiles:
                    xtile = xt[tt % 2]
                    xk = K("xt", tt % 2)
                    self.dma("sp", xtile[0:n, :], x_src(l, tt), r=[xkey(tt)], w=[xk])
                    norm_transpose(n, xtile, xk, xnT, lc, K("xnT"), 0, xnb, "A")

                def gcol(c0):
                    return gc0 + c0 if c0 < 512 else TP

                wb, wkey = load_wblock(w_in[l, :, 0:512])
                for ct in range(4):
                    for (c0, ncol) in subs:
                        bank = nb()
                        proj_F(wb, wkey, ct, c0, ncol, bank)
                        zt = zf[zi[0] % 2]
                        zk = K("zf", zi[0] % 2)
                        zi[0] += 1
                        self.act(lambda e, zt=zt, ncol=ncol, bank=bank: e.activation(
                            out=zt[:, 0:ncol], in_=PS[bank][:, 0:ncol], func=AF.Copy), r=[K("ps", bank)], w=[zk])
                        if c0 < 512:
                            self.dma("sp", xa_d[ct, :, gc0:gc0 + 512], zt[:, 0:512], r=[zk], w=[K("xa_d", ct)])
                        else:
                            self.dma("sp", xa_sd[ct, :, :], zt[:, 0:TS], r=[zk], w=[K("xa_sd", ct)])
                wb, wkey = load_wblock(w_in[l, :, 512:1024])
                for ct in range(4):
                    for (c0, ncol) in subs:
                        bank = nb()
                        proj_F(wb, wkey, ct, c0, ncol, bank)
                        g0_ = gcol(c0)
                        gelu(PS[bank][:, 0:ncol], ycat[:, ct, g0_:g0_ + ncol], 128, ncol, [K("ps", bank)],
                             [K("ycat", "a", ct)])
                wb, wkey = load_wblock(w_in[l, :, 4096:4608])
                for ct in range(4):
                    for (c0, ncol) in subs:
                        bank = nb()
                        proj_F(wb, wkey, ct, c0, ncol, bank)
                        gelu(PS[bank][:, 0:ncol], uact[:, ct, c0:c0 + ncol], 128, ncol, [K("ps", bank)],
                             [K("uact")])
                wb, wkey = load_wblock(w_in[l, :, 4608:5120])
                for ti, (tt, lc, n) in enumerate(tiles):
                    bank = nb()
                    proj_T(wb, wkey, lc, n, bank)
                    g2 = gel[2]
                    gelu(PS[bank][0:n, :], g2[0:n, :], n, 512, [K("ps", bank)], [K("gel2")])
                    self.dve(lambda e, n=n: e.bn_stats(out=small[0:n, 8:14], in_=g2[0:n, :]), r=[K("gel2")],
                             w=[K("bn")])
                    self.dve(lambda e, n=n: e.bn_aggr(out=small[0:n, 14:16], in_=small[0:n, 8:14]), r=[K("bn")],
                             w=[K("bn2")])
                    self.act(lambda e, n=n: e.activation(out=small[0:n, 16:17], in_=small[0:n, 15:16], func=AF.Ln,
                                                         scale=1.0, bias=EPS), r=[K("bn2")], w=[K("bn3")])
                    self.act(lambda e, n=n: e.activation(out=small[0:n, 16:17], in_=small[0:n, 16:17], func=AF.Exp,
                                                         scale=-0.5), r=[K("bn3")], w=[K("bn3")])
                    self.dve(lambda e, n=n: e.tensor_scalar(out=g2[0:n, :], in0=g2[0:n, :], scalar1=small[0:n, 14:15],
                                                            scalar2=small[0:n, 16:17], op0=ALU.subtract,
                                                            op1=ALU.mult),
                             r=[K("gel2"), K("bn2"), K("bn3")], w=[K("gel2")])
                    self.dve(lambda e, n=n: e.tensor_tensor(out=g2[0:n, :], in0=g2[0:n, :], in1=mlpv[0:n, 0:512],
                                                            op=ALU.mult), r=[K("gel2"), K("mlpv")], w=[K("gel2")])
                    self.dve(lambda e, n=n: e.tensor_tensor(out=g2[0:n, :], in0=g2[0:n, :], in1=mlpv[0:n, 512:1024],
                                                            op=ALU.add), r=[K("gel2"), K("mlpv")], w=[K("gel2")])
                    if tt == NPT:
                        self.dma("sp", cv_s[l], g2[0:n, :], r=[K("gel2")], w=[K("cv_s")])
                    self.dve(lambda e, n=n, ti=ti: e.tensor_copy(out=vnb[0:n, ti, :], in_=g2[0:n, :]), r=[K("gel2")],
                             w=[K("vnb", ti)])
                    bank2 = nb()
                    for g in range(4):
                        self.pe(lambda e, g=g, n=n, ti=ti, bank2=bank2: e.matmul(
                            PS[bank2][:, g * 128:g * 128 + n], lhsT=vnb[0:n, ti, g * 128:(g + 1) * 128],
                            rhs=wspT[0:n, g, 0:n], start=True, stop=True),
                            r=[K("vnb", ti), K("wspT")], w=[K("ps", bank2)])
                    t0 = gel[0]
                    g0_ = gcol(lc)
                    self.dve(lambda e, n=n, bank2=bank2, t0=t0: e.tensor_tensor(
                        out=t0.rearrange("p (g q) -> p g q", g=4)[:, :, 0:n],
                        in0=PS[bank2].rearrange("p (g q) -> p g q", g=4)[:, :, 0:n],
                        in1=mlpv[:, 1024:1536].rearrange("p (g q) -> p g q", g=4)[:, :, 0:n], op=ALU.add),
                        r=[K("ps", bank2), K("mlpv")], w=[K("gel", 0)])
                    self.dve(lambda e, n=n, lc=lc, g0_=g0_, t0=t0: e.tensor_tensor(
                        out=ycat[:, 12:16, g0_:g0_ + n], in0=t0.rearrange("p (g q) -> p g q", g=4)[:, :, 0:n],
                        in1=uact[:, :, lc:lc + n], op=ALU.mult),
                        r=[K("gel", 0), K("uact")], w=[K("ycat", "c")])
                for blk in range(6):
                    col0 = 1024 + blk * 512
                    wb, wkey = load_wblock(w_in[l, :, col0:col0 + 512])
                    kind = blk // 2
                    hb = (blk % 2) * 4
                    for (tt, lc, n) in tiles:
                        bank = nb()
                        proj_T(wb, wkey, lc, n, bank)
                        zt = zf[zi[0] % 2]
                        zk = K("zf", zi[0] % 2)
                        zi[0] += 1
                        self.act(lambda e, zt=zt, n=n, bank=bank: e.activation(out=zt[0:n, :], in_=PS[bank][0:n, :],
                                                                             func=AF.Copy), r=[K("ps", bank)], w=[zk])
                        if kind < 2:
                            rope(zt, zk, n, tt)
                        if kind >= 1:
                            dst = (k_p if kind == 1 else v_p) if tt < NPT else (k_s if kind == 1 else v_s)
                            r0, r1 = (128 * tt, 128 * tt + 128) if tt < NPT else (0, TS)
                            self.dma("sp", dst[l, r0:r1, hb * 128:hb * 128 + 512], zt[0:n, :], r=[zk],
                                     w=[K("kvout", kind, blk % 2, tt)])
                        if kind < 2:
                            tb_ = 4 + (zi[0] % 4)
                            for hh in range(4):
                                self.pe(lambda e, zt=zt, n=n, hh=hh, tb_=tb_: e.transpose(
                                    out=PS[tb_][:, hh * 128:hh * 128 + n], in_=zt[0:n, hh * 128:(hh + 1) * 128],
                                    identity=identF[0:n, 0:n]), r=[zk, K("identF")], w=[K("ps", tb_)])
                            psv = PS[tb_].rearrange("p (h t) -> p h t", h=4)[:, :, 0:n]
                            if kind == 1 and tt == NPT:
                                self.act(lambda e, psv=psv, hb=hb: e.activation(out=ksT[:, hb:hb + 4, :], in_=psv,
                                                                                func=AF.Copy),
                                         r=[K("ps", tb_)], w=[K("ksT")])
                            else:
                                st_ = stg[zi[0] % 2]
                                sk = K("stg", zi[0] % 2)
                                stv = st_.rearrange("p (h t) -> p h t", h=4)[:, :, 0:n]
                                self.act(lambda e, psv=psv, stv=stv: e.activation(out=stv, in_=psv, func=AF.Copy),
                                         r=[K("ps", tb_)], w=[sk])
                                if kind == 0:
                                    g0_ = gcol(lc)
                                    self.dma("sp", QT_d[hb:hb + 4, :, g0_:g0_ + n].rearrange("h e t -> e h t"), stv,
                                             r=[sk], w=[K("QT_d", hb // 4)])
                                else:
                                    for hp in range(2):
                                        ktd = KT_own[(hb // 2) + hp]
                                        self.dma("sp",
                                                 ktd[:, 128 * tt:128 * tt + 128].rearrange("(h e) t -> e h t", h=2),
                                                 stv[:, 2 * hp:2 * hp + 2, :], r=[sk], w=[K("KT_own", (hb // 2) + hp)])
                        else:
                            if tt < NPT:
                                st_ = stg[zi[0] % 2]
                                sk = K("stg", zi[0] % 2)
                                self.dve(lambda e, zt=zt, st_=st_: e.tensor_copy(out=st_, in_=zt), r=[zk], w=[sk])
                                vd = V_own[tt // 4]
                                r0 = (tt % 4) * 128
                                self.dma("sp", vd[r0:r0 + 128, hb * 128:hb * 128 + 512], st_, r=[sk],
                                         w=[K("V_own", tt // 4)])
                            else:
                                self.dve(lambda e, zt=zt, n=n, hb=hb: e.tensor_copy(
                                    out=vsB[0:n, hb * 128:hb * 128 + 512], in_=zt[0:n, :]), r=[zk], w=[K("vsB")])
            for i in range(4):
                allgather(KT_own[i], KT_all[i], [K("KT_own", i)], [K("KT_all", i)])
            for i in range(NVC):
                allgather(V_own[i], V_all[i], [K("V_own", i)], [K("V_all", i)])

            fence()
            xadk = [K("xa_d", c) for c in range(4)]
            self.dma("sp", halo_in[:, :].rearrange("p (c j) -> p c j", c=4),
                     xa_d[:, :, TP - 3:TP].rearrange("c p j -> p c j"), r=xadk, w=[K("halo_in")])
            allgather(halo_in, halo_all, [K("halo_in")], [K("halo_all")])
            self.dma("sp", c_p[l], xa_d[:, :, TP - 3:TP].rearrange("c p j -> p c j"), r=xadk, w=[K("c_p")])
            self.dma("sp", c_s[l], xa_sd[:, :, TS - 3:TS].rearrange("c p j -> p c j"),
                     r=[K("xa_sd", c) for c in range(4)], w=[K("c_s")])
            self.dma("sp", halo_sb, halo_all.rearrange("(r p) c -> p r c", p=128), r=[K("halo_all")],
                     w=[K("halo_sb")])
            fchunks = [(c, 512) for c in range(0, TP, 512)] + [(TP, TS)]
            SO = 3 + TP
            for ct in range(4):
                xak = K("xa")
                self.dma("sp", xa[:, 3:3 + TP], xa_d[ct], r=[K("xa_d", ct)], w=[xak])
                self.dma("sp", xa[:, SO + 3:SO + 3 + TS], xa_sd[ct], r=[K("xa_sd", ct)], w=[xak])
                self.dve(lambda e, ct=ct: e.tensor_scalar(out=xa[:, 0:3], in0=halo_sb[:, 0, ct * 3:ct * 3 + 3],
                                                          scalar1=cc[:, 3:4], scalar2=None, op0=ALU.mult),
                         r=[K("halo_sb"), K("cc"), xak], w=[xak])
                for r_ in range(1, 4):
                    self.dve(lambda e, r_=r_, ct=ct: e.scalar_tensor_tensor(
                        out=xa[:, 0:3], in0=halo_sb[:, r_, ct * 3:ct * 3 + 3], scalar=cc[:, 3 + r_:4 + r_],
                        in1=xa[:, 0:3], op0=ALU.mult, op1=ALU.add), r=[K("halo_sb"), K("cc"), xak], w=[xak])
                self.dve(lambda e, ct=ct: e.tensor_copy(out=xa[:, SO:SO + 3], in_=pv[:, 69 + ct * 3:72 + ct * 3]),
                         r=[K("pv"), xak], w=[xak])
                for (ro, T, xo) in ((0, TP, 0), (SO, TS, TP)):
                    self.dve(lambda e, ro=ro, T=T, xo=xo, ct=ct: e.tensor_scalar(
                        out=xc[:, xo:xo + T], in0=xa[:, ro:ro + T], scalar1=pv[:, 32 + ct * 4:33 + ct * 4],
                        scalar2=pv[:, 48 + ct:49 + ct], op0=ALU.mult, op1=ALU.add),
                        r=[xak, K("pv")], w=[K("xc")])
                    for j in range(1, 4):
                        self.dve(lambda e, ro=ro, T=T, xo=xo, ct=ct, j=j: e.scalar_tensor_tensor(
                            out=xc[:, xo:xo + T], in0=xa[:, ro + j:ro + j + T],
                            scalar=pv[:, 32 + ct * 4 + j:33 + ct * 4 + j], in1=xc[:, xo:xo + T],
                            op0=ALU.mult, op1=ALU.add), r=[xak, K("pv"), K("xc")], w=[K("xc")])
                self.dve(lambda e: e.tensor_copy(out=xcb, in_=xc), r=[K("xc")], w=[K("xcb")])
                for (c0, ncol) in fchunks:
                    ba, bx = nb(), nb()
                    self.pe(lambda e, ct=ct, c0=c0, ncol=ncol, ba=ba: e.matmul(
                        PS[ba][:, 0:ncol], lhsT=wrg[:, ct, :], rhs=xcb[:, c0:c0 + ncol], start=True, stop=True),
                        r=[K("wrg"), K("xcb")], w=[K("ps", ba)])
                    self.pe(lambda e, ct=ct, c0=c0, ncol=ncol, bx=bx: e.matmul(
                        PS[bx][:, 0:ncol], lhsT=wrg[:, 4 + ct, :], rhs=xcb[:, c0:c0 + ncol], start=True, stop=True),
                        r=[K("wrg"), K("xcb")], w=[K("ps", bx)])
                    er, ei, av, a2, sq, tb = [t[:, 0:ncol] for t in lr]
                    self.act(lambda e, ba=ba, ncol=ncol, ct=ct, er=er: e.activation(
                        out=er, in_=PS[ba][:, 0:ncol], func=AF.Exp, scale=-1.0, bias=pv2[:, ct:ct + 1]),
                        r=[K("ps", ba), K("pv2a")], w=[K("lr", 0)])
                    self.act(lambda e, bx=bx, ncol=ncol, ct=ct, ei=ei: e.activation(
                        out=ei, in_=PS[bx][:, 0:ncol], func=AF.Exp, scale=-1.0, bias=pv2[:, 4 + ct:5 + ct]),
                        r=[K("ps", bx), K("pv2a")], w=[K("lr", 1)])
                    self.dve(lambda e, er=er: e.tensor_scalar(out=er, in0=er, scalar1=1.0, scalar2=None, op0=ALU.add),
                             r=[K("lr", 0)], w=[K("lr", 0)])
                    self.dve(lambda e, er=er: e.reciprocal(out=er, in_=er), r=[K("lr", 0)], w=[K("lr", 0)])
                    self.dve(lambda e, ei=ei: e.tensor_scalar(out=ei, in0=ei, scalar1=1.0, scalar2=None, op0=ALU.add),
                             r=[K("lr", 1)], w=[K("lr", 1)])
                    self.dve(lambda e, ei=ei: e.reciprocal(out=ei, in_=ei), r=[K("lr", 1)], w=[K("lr", 1)])
                    if c0 < TP:
                        a_dst = la[0][:, c0:c0 + ncol]
                        b_dst = la[1][:, c0:c0 + ncol]
                        ak = [K("la")]
                    else:
                        a_dst = las[:, ct, 0, :]
                        b_dst = las[:, ct, 1, :]
                        ak = [K("las", ct)]
                    self.dve(lambda e, av=av, er=er, ct=ct: e.tensor_scalar(
                        out=av, in0=er, scalar1=pv2[:, 8 + ct:9 + ct], scalar2=None, op0=ALU.mult),
                        r=[K("lr", 0), K("pv2cl")], w=[K("lr", 2)])
                    self.act(lambda e, av=av, a_dst=a_dst: e.activation(out=a_dst, in_=av, func=AF.Exp),
                             r=[K("lr", 2)], w=ak)
                    self.act(lambda e, av=av, a2=a2: e.activation(out=a2, in_=av, func=AF.Exp, scale=2.0),
                             r=[K("lr", 2)], w=[K("lr", 3)])
                    self.act(lambda e, a2=a2, sq=sq: e.activation(out=sq, in_=a2, func=AF.Ln, scale=-1.0, bias=1.0),
                             r=[K("lr", 3)], w=[K("lr", 4)])
                    self.act(lambda e, sq=sq: e.activation(out=sq, in_=sq, func=AF.Exp, scale=0.5),
                             r=[K("lr", 4)], w=[K("lr", 4)])
                    self.dve(lambda e, sq=sq, ei=ei, tb=tb: e.tensor_tensor(out=tb, in0=sq, in1=ei, op=ALU.mult),
                             r=[K("lr", 4), K("lr", 1)], w=[K("lr", 5)])
                    self.dve(lambda e, tb=tb, c0=c0, ncol=ncol, b_dst=b_dst: e.tensor_tensor(
                        out=b_dst, in0=tb, in1=xc[:, c0:c0 + ncol], op=ALU.mult),
                        r=[K("lr", 5), K("xc")], w=ak)
                self.dve(lambda e: e.tensor_tensor_scan(out=hbuf, data0=la[0], data1=la[1], initial=0.0,
                                                        op0=ALU.mult, op1=ALU.add), r=[K("la")], w=[K("hbuf")])
                self.dve(lambda e, ct=ct: e.tensor_copy(out=ab_st[:, 4 + ct:5 + ct], in_=hbuf[:, TP - 1:TP]),
                         r=[K("hbuf")], w=[K("ab_st")])
                self.dve(lambda e: e.memset(xc[:, 0:TP], 0.0), r=[K("xc")], w=[K("xc")])
                self.dve(lambda e: e.tensor_tensor_scan(out=hbuf, data0=la[0], data1=xc[:, 0:TP], initial=1.0,
                                                        op0=ALU.mult, op1=ALU.add), r=[K("la"), K("xc")],
                         w=[K("hbuf")])
                self.dve(lambda e, ct=ct: e.tensor_copy(out=ab_st[:, ct:ct + 1], in_=hbuf[:, TP - 1:TP]),
                         r=[K("hbuf")], w=[K("ab_st")])
                self.dma("sp", lru_a[ct], la[0], r=[K("la")], w=[K("lru_a", ct)])
                self.dma("sp", lru_b[ct], la[1], r=[K("la")], w=[K("lru_b", ct)])
            self.dma("sp", ab_in[:, :], ab_st, r=[K("ab_st")], w=[K("ab_in")])
            allgather(ab_in, ab_all, [K("ab_in")], [K("ab_all")])
            self.dma("sp", ab_sb, ab_all.rearrange("(r p) c -> p r c", p=128), r=[K("ab_all")], w=[K("ab_sb")])
            hin = small[:, 24:28]
            Hs = small[:, 28:32]
            self.dve(lambda e: e.memset(small[:, 24:32], 0.0), w=[K("hin")])
            for s_ in range(3):
                self.dve(lambda e, s_=s_: e.tensor_tensor(out=Hs, in0=Hs, in1=ab_sb[:, s_, 0:4], op=ALU.mult),
                         r=[K("hin"), K("ab_sb")], w=[K("hin")])
                self.dve(lambda e, s_=s_: e.tensor_tensor(out=Hs, in0=Hs, in1=ab_sb[:, s_, 4:8], op=ALU.add),
                         r=[K("hin"), K("ab_sb")], w=[K("hin")])
                self.dve(lambda e, s_=s_: e.scalar_tensor_tensor(out=hin, in0=Hs, scalar=cc[:, 8 + s_:9 + s_],
                                                                 in1=hin, op0=ALU.mult, op1=ALU.add),
                         r=[K("hin"), K("cc")], w=[K("hin")])
            for ct in range(4):
                self.dma("sp", la[0], lru_a[ct], r=[K("lru_a", ct)], w=[K("la")])
                self.dma("sp", la[1], lru_b[ct], r=[K("lru_b", ct)], w=[K("la")])
                self.dve(lambda e, ct=ct: e.tensor_tensor_scan(out=hbuf, data0=la[0], data1=la[1],
                                                               initial=small[:, 24 + ct:25 + ct],
                                                               op0=ALU.mult, op1=ALU.add),
                         r=[K("la"), K("hin")], w=[K("hbuf")])
                self.dve(lambda e, ct=ct: e.tensor_copy(out=small[:, 32 + ct:33 + ct], in_=hbuf[:, TP - 1:TP]),
                         r=[K("hbuf")], w=[K("hlast")])
                self.dve(lambda e, ct=ct: e.tensor_tensor(out=ycat[:, ct, 0:TP], in0=hbuf, in1=ycat[:, ct, 0:TP],
                                                          op=ALU.mult), r=[K("hbuf"), K("ycat", "a", ct)],
                         w=[K("ycat", "a", ct)])
                self.dve(lambda e, ct=ct: e.tensor_tensor_scan(out=hbuf[:, 0:TS], data0=las[:, ct, 0, :],
                                                               data1=las[:, ct, 1, :], initial=pv[:, 65 + ct:66 + ct],
                                                               op0=ALU.mult, op1=ALU.add),
                         r=[K("las", ct), K("pv"), K("hbuf")], w=[K("hbuf")])
                self.dve(lambda e, ct=ct: e.tensor_copy(out=small[:, 36 + ct:37 + ct], in_=hbuf[:, TS - 1:TS]),
                         r=[K("hbuf")], w=[K("hlast")])
                self.dve(lambda e, ct=ct: e.tensor_tensor(out=ycat[:, ct, TP:TP + TS], in0=hbuf[:, 0:TS],
                                                          in1=ycat[:, ct, TP:TP + TS], op=ALU.mult),
                         r=[K("hbuf"), K("ycat", "a", ct)], w=[K("ycat", "a", ct)])
            self.dma("sp", h_p[l], small[:, 32:36], r=[K("hlast")], w=[K("h_p")])
            self.dma("sp", h_s[l], small[:, 36:40], r=[K("hlast")], w=[K("h_s")])

            fence()

            def attend(h, qT, qk, q0, nq, ktiles, out_col, rkeys):
                NKTL = len(ktiles)

                def s_mm(i):
                    kT, v_, nk, bias, m = ktiles[i]
                    sb = 2 * (i % 2)
                    self.pe(lambda e: e.matmul(PS[sb][0:nk, 0:nq], lhsT=kT[0:64, :], rhs=qT[0:64, q0:q0 + nq],
                                               start=True, stop=True), r=rkeys + [qk], w=[K("ps", sb)])
                    self.pe(lambda e: e.matmul(PS[sb + 1][0:nk, 0:nq], lhsT=kT[64:128, :], rhs=qT[64:128, q0:q0 + nq],
                                               start=True, stop=True), r=rkeys + [qk], w=[K("ps", sb + 1)])

                def p_mm(i):
                    kT, v_, nk, bias, m = ktiles[i]
                    sb = 2 * (i % 2)
                    for mp in range(2):
                        pb = Pb[(2 * i + mp) % 4]
                        pk = K("Pb", (2 * i + mp) % 4)
                        self.act(lambda e, mp=mp, pb=pb: e.activation(out=pb[0:nk, 0:nq], in_=PS[sb + mp][0:nk, 0:nq],
                                                                      func=AF.Exp, scale=0.125, bias=bias),
                                 r=[K("ps", sb + mp), K("cc")], w=[pk])
                        if m is not None:
                            self.dve(lambda e, pb=pb: e.tensor_tensor(out=pb[0:nk, 0:nq], in0=pb[0:nk, 0:nq],
                                                                      in1=maskB[0:nk, m, 0:nq], op=ALU.mult),
                                     r=[pk, K("maskB")], w=[pk])
                        ob, lb = 4 + 2 * mp, 5 + 2 * mp
                        self.pe(lambda e, pb=pb, ob=ob: e.matmul(PS[ob][:, 0:nq], lhsT=v_, rhs=pb[0:nk, 0:nq],
                                                                 start=(i == 0), stop=(i == NKTL - 1)),
                                r=rkeys + [pk], w=[K("ps", ob)])
                        self.pe(lambda e, pb=pb, lb=lb: e.matmul(PS[lb][:, 0:nq], lhsT=onesB[0:nk, :],
                                                                 rhs=pb[0:nk, 0:nq], start=(i == 0),
                                                                 stop=(i == NKTL - 1)),
                                r=[pk, K("onesB")], w=[K("ps", lb)])

                s_mm(0)
                for i in range(NKTL):
                    if i + 1 < NKTL:
                        s_mm(i + 1)
                    p_mm(i)
                r1, r2, t1, t2, o_, sqv = [t[:, 0:nq] for t in fin]
                fk = [K("fin", i) for i in range(6)]
                self.dve(lambda e: e.reciprocal(out=r1, in_=PS[5][:, 0:nq]), r=[K("ps", 5)], w=[fk[0]])
                self.dve(lambda e: e.reciprocal(out=r2, in_=PS[7][:, 0:nq]), r=[K("ps", 7)], w=[fk[1]])
                self.dve(lambda e: e.tensor_tensor(out=t1, in0=PS[4][:, 0:nq], in1=r1, op=ALU.mult),
                         r=[K("ps", 4), fk[0]], w=[fk[2]])
                self.dve(lambda e: e.tensor_tensor(out=t2, in0=PS[6][:, 0:nq], in1=r2, op=ALU.mult),
                         r=[K("ps", 6), fk[1]], w=[fk[3]])
                self.dve(lambda e: e.scalar_tensor_tensor(out=o_, in0=t2, scalar=pv2[:, 13:14], in1=t1,
                                                          op0=ALU.mult, op1=ALU.add),
                         r=[fk[2], fk[3], K("pv2nl")], w=[fk[4]])
                self.act(lambda e: e.activation(out=sqv, in_=o_, func=AF.Square), r=[fk[4]], w=[fk[5]])
                self.pe(lambda e: e.matmul(PS[0][:, 0:nq], lhsT=onesF, rhs=sqv, start=True, stop=True),
                        r=[fk[5], K("onesF")], w=[K("ps", 0)])
                self.act(lambda e: e.activation(out=r2, in_=PS[0][:, 0:nq], func=AF.Ln, scale=1.0 / 128, bias=EPS),
                         r=[K("ps", 0)], w=[fk[1]])
                self.act(lambda e: e.activation(out=r2, in_=r2, func=AF.Exp, scale=-0.5), r=[fk[1]], w=[fk[1]])
                self.dve(lambda e: e.scalar_tensor_tensor(out=ycat[:, 4 + h, out_col:out_col + nq], in0=o_,
                                                          scalar=pv2[:, 12:13], in1=r2, op0=ALU.mult, op1=ALU.mult),
                         r=[fk[4], fk[1], K("pv2gs")], w=[K("ycat", "b", h)])

            for h in range(8):
                hp, hh = h // 2, h % 2
                sl = h % 2
                ktk, vk, qk = K("KTh", sl), K("Vh", sl), K("QTh", sl)
                self.dma("sp", QTh[sl], QT_d[h], r=[K("QT_d", h // 4)], w=[qk])
                for s_ in range(3):
                    self.dma("sp", KTh[sl][:, s_ * TP:(s_ + 1) * TP],
                             KT_all[hp][s_ * 256 + hh * 128:s_ * 256 + hh * 128 + 128, :],
                             r=[K("KT_all", hp)], w=[ktk])
                    for tc in range(NVC):
                        self.dma("sp", Vh[sl][:, s_ * NKT + tc * 4:s_ * NKT + tc * 4 + 4, :],
                                 V_all[tc][s_ * 512:(s_ + 1) * 512, h * 128:(h + 1) * 128].rearrange(
                                     "(i p) d -> p i d", p=128), r=[K("V_all", tc)], w=[vk])
                self.dma("sp", KTh[sl][:, 3 * TP:4 * TP], KT_own[hp][hh * 128:hh * 128 + 128, :],
                         r=[K("KT_own", hp)], w=[ktk])
                for tc in range(NVC):
                    self.dma("sp", Vh[sl][:, 3 * NKT + tc * 4:3 * NKT + tc * 4 + 4, :],
                             V_own[tc][:, h * 128:(h + 1) * 128].rearrange("(i p) d -> p i d", p=128),
                             r=[K("V_own", tc)], w=[vk])
                self.dma("pool", kc, ck[l, :, h * 128:(h + 1) * 128].rearrange("(kt p) e -> p kt e", p=128),
                         w=[K("kc")])
                self.dma("pool", vc[:, 0:PKT, :], cv[l, :, h * 128:(h + 1) * 128].rearrange("(kt p) e -> p kt e", p=128),
                         w=[K("vc")])
                self.dve(lambda e, h=h: e.tensor_copy(out=vc[0:TS, PKT, :], in_=vsB[0:TS, h * 128:(h + 1) * 128]),
                         r=[K("vsB"), K("vc")], w=[K("vc")])
                for qb in range(NCH):
                    kts = []
                    for s_ in range(3):
                        for kt in range(NKT):
                            c_ = s_ * TP + kt * 128
                            kts.append((KTh[sl][:, c_:c_ + 128], Vh[sl][:, s_ * NKT + kt, :], 128, cc[:, s_:s_ + 1],
                                        None))
                    for kt in range(4 * qb + 4):
                        c_ = 3 * TP + kt * 128
                        m = kt - 4 * qb
                        kts.append((KTh[sl][:, c_:c_ + 128], Vh[sl][:, 3 * NKT + kt, :], 128, cc[:, 11:12],
                                    m if m >= 0 else None))
                    attend(h, QTh[sl], qk, qb * 512, 512, kts, qb * 512, [ktk, vk])
                for g4 in range(PKT // 4):
                    bank = g4 % 4
                    for i in range(4):
                        kt = g4 * 4 + i
                        self.pe(lambda e, kt=kt, i=i, bank=bank: e.transpose(
                            out=psB(bank)[:, i * 128:(i + 1) * 128], in_=kc[:, kt, :], identity=identB),
                            r=[K("kc"), K("identB")], w=[K("ps", bank)])
                    self.act(lambda e, g4=g4, bank=bank: e.activation(out=KTc[:, g4 * 512:(g4 + 1) * 512],
                                                                     in_=psB(bank)[:, 0:512], func=AF.Copy),
                             r=[K("ps", bank)], w=[K("KTc")])
                self.dve(lambda e, h=h: e.tensor_copy(out=KTc[:, PAST:PAST + TS], in_=ksT[:, h, :]),
                         r=[K("ksT"), K("KTc")], w=[K("KTc")])
                kts = []
                for kt in range(PKT):
                    kts.append((KTc[:, kt * 128:(kt + 1) * 128], vc[:, kt, :], 128, cc[:, 11:12], None))
                kts.append((KTc[:, PAST:PAST + TS], vc[0:TS, PKT, :], TS, cc[0:TS, 11:12], None))
                attend(h, QTh[sl], qk, TP, TS, kts, TP, [K("KTc"), K("vc")])

            fence()
            if dbg_ycat is not None and l == 0:
                self.dma("sp", dbg_ycat[:, :, :], ycat, r=[K("ycat", "c")], w=[K("dbg")])
                fence()
            for cb in range(4):
                self.dma("pool", wout_sb[:, :, cb * 512:(cb + 1) * 512],
                         w_out[l, :, cb * 512:(cb + 1) * 512].rearrange("(kt p) c -> p kt c", p=128),
                         w=[K("wout", cb)])
            self.dma("sp", gpostA, bvec[l:l + 1, 0:2048].partition_broadcast(128), w=[K("gpostA")])
            ycat_keys = [K("ycat", "a", c) for c in range(4)] + [K("ycat", "b", h) for h in range(8)] + [K("ycat", "c")]
            for tt in range(NT):
                n = tokn(tt)
                c0 = tcol(tt)
                xtile = wxt[tt % 2]
                xk = K("wxt", tt % 2)
                self.dma("sp", xtile[0:n, :], x_src(l, tt), r=[xkey(tt)], w=[xk])
                for cb in range(4):
                    bank = cb + 4 * (tt % 2)
                    for ct in range(16):
                        self.pe(lambda e, ct=ct, cb=cb, bank=bank: e.matmul(
                            PS[bank][0:n, :], lhsT=ycat[:, ct, c0:c0 + n], rhs=wout_sb[:, ct, cb * 512:(cb + 1) * 512],
                            start=(ct == 0), stop=(ct == 15)), r=ycat_keys + [K("wout", cb)], w=[K("ps", bank)])
                    self.act(lambda e, bank=bank, cb=cb: e.activation(out=wjunk[0:n, :], in_=PS[bank][0:n, :],
                                                                    func=AF.Square,
                                                                    accum_out=small[0:n, 40 + cb:41 + cb]),
                             r=[K("ps", bank)], w=[K("wjunk"), K("wss", cb)])
                self.dve(lambda e: e.tensor_reduce(out=small[0:n, 44:45], in_=small[0:n, 40:44], axis=X, op=ALU.add),
                         r=[K("wss", c) for c in range(4)], w=[K("wss4")])
                rstd_from_ss(small[0:n, 44:45], small[0:n, 45:46], D, [K("wss4")], [K("wrstd")])
                for cb in range(4):
                    bank = cb + 4 * (tt % 2)
                    tmp = wtmp[cb]
                    self.dve(lambda e, bank=bank, cb=cb, tmp=tmp: e.scalar_tensor_tensor(
                        out=tmp[0:n, :], in0=PS[bank][0:n, :], scalar=small[0:n, 45:46],
                        in1=gpostA[0:n, cb * 512:(cb + 1) * 512], op0=ALU.mult, op1=ALU.mult),
                        r=[K("ps", bank), K("wrstd"), K("gpostA")], w=[K("wtmp", cb)])
                    self.dve(lambda e, cb=cb, tmp=tmp: e.tensor_tensor(
                        out=xtile[0:n, cb * 512:(cb + 1) * 512], in0=xtile[0:n, cb * 512:(cb + 1) * 512],
                        in1=tmp[0:n, :], op=ALU.add), r=[K("wtmp", cb), xk], w=[xk])
                self.dma("sp", xres_ap(tt), xtile[0:n, :], r=[xk], w=[xkey(tt)])

            fence()
            self.dma("sp", gpostF, bvec[l:l + 1, 2048:4096].partition_broadcast(128), w=[K("gpostF")])
            gu_n = [0]
            wd_n = [0]
            for ci in range(NCH):
                tiles = chunk_tiles(ci)
                subs = chunk_subs(ci)
                for (tt, lc, n) in tiles:
                    xtile = fxt[tt % 2]
                    xk = K("fxt", tt % 2)
                    self.dma("sp", xtile[0:n, :], xres_ap(tt), r=[xkey(tt)], w=[xk])
                    norm_transpose(n, xtile, xk, hnT, lc, K("hnT"), 16, fxnb, "F")
                for fb in range(FT // 2):
                    sg = gub[gu_n[0] % 4]
                    kg = K("gub", gu_n[0] % 4)
                    gu_n[0] += 1
                    su = gub[gu_n[0] % 4]
                    ku = K("gub", gu_n[0] % 4)
                    gu_n[0] += 1
                    self.dma("pool", sg, w_gate[l, :, fb * 256:(fb + 1) * 256].rearrange("(kt p) c -> p kt c", p=128),
                             w=[kg])
                    self.dma("pool", su, w_up[l, :, fb * 256:(fb + 1) * 256].rearrange("(kt p) c -> p kt c", p=128),
                             w=[ku])
                    for fi in range(2):
                        f = 2 * fb + fi
                        for (c0, ncol) in subs:
                            bg = bankrr[0] % 8
                            bu = (bankrr[0] + 1) % 8
                            bankrr[0] += 2
                            for dt_ in range(DT):
                                self.pe(lambda e, dt_=dt_, bg=bg: e.matmul(
                                    PS[bg][:, 0:ncol], lhsT=sg[:, dt_, fi * 128:(fi + 1) * 128],
                                    rhs=hnT[:, dt_, c0:c0 + ncol], start=(dt_ == 0), stop=(dt_ == DT - 1)),
                                    r=[kg, K("hnT")], w=[K("ps", bg)])
                            for dt_ in range(DT):
                                self.pe(lambda e, dt_=dt_, bu=bu: e.matmul(
                                    PS[bu][:, 0:ncol], lhsT=su[:, dt_, fi * 128:(fi + 1) * 128],
                                    rhs=hnT[:, dt_, c0:c0 + ncol], start=(dt_ == 0), stop=(dt_ == DT - 1)),
                                    r=[ku, K("hnT")], w=[K("ps", bu)])
                            ti_ = (bankrr[0] // 2) % 2
                            e_t = ft_[2 * ti_][:, 0:ncol]
                            t_t = ft_[2 * ti_ + 1][:, 0:ncol]
                            ek, tk = K("ft", 2 * ti_), K("ft", 2 * ti_ + 1)
                            self.act(lambda e, bg=bg, e_t=e_t: e.activation(out=e_t, in_=PS[bg][:, 0:ncol],
                                                                            func=AF.Exp, scale=-1.0),
                                     r=[K("ps", bg)], w=[ek])
                            self.dve(lambda e, e_t=e_t: e.tensor_scalar(out=e_t, in0=e_t, scalar1=1.0, scalar2=None,
                                                                        op0=ALU.add), r=[ek], w=[ek])
                            self.dve(lambda e, e_t=e_t: e.reciprocal(out=e_t, in_=e_t), r=[ek], w=[ek])
                            self.dve(lambda e, bg=bg, e_t=e_t, t_t=t_t: e.tensor_tensor(
                                out=t_t, in0=PS[bg][:, 0:ncol], in1=e_t, op=ALU.mult), r=[K("ps", bg), ek], w=[tk])
                            self.dve(lambda e, bu=bu, t_t=t_t, f=f: e.tensor_tensor(
                                out=ffT[:, f, c0:c0 + ncol], in0=PS[bu][:, 0:ncol], in1=t_t, op=ALU.mult),
                                r=[K("ps", bu), tk], w=[K("ffT")])
                for cb in range(4):
                    for q in range(4):
                        sw = wdb[wd_n[0] % 2]
                        kw = K("wdb", wd_n[0] % 2)
                        wd_n[0] += 1
                        self.dma("pool", sw, w_down[l, q * FQ * 128:(q + 1) * FQ * 128,
                                                    cb * 512:(cb + 1) * 512].rearrange("(f p) c -> p f c", p=128),
                                 w=[kw])
                        for ti, (tt, lc, n) in enumerate(tiles):
                            for f in range(FQ):
                                self.pe(lambda e, f=f, ti=ti, lc=lc, n=n, q=q: e.matmul(
                                    PS[ti][0:n, :], lhsT=ffT[:, q * FQ + f, lc:lc + n], rhs=sw[:, f, :],
                                    start=(q == 0 and f == 0), stop=(q == 3 and f == FQ - 1)),
                                    r=[K("ffT"), kw], w=[K("ps", ti)])
                    for ti, (tt, lc, n) in enumerate(tiles):
                        ys = ystg[(cb * 5 + ti) % 2]
                        yk = K("ystg", (cb * 5 + ti) % 2)
                        self.act(lambda e, ti=ti, n=n, ys=ys: e.activation(out=ys[0:n, :], in_=PS[ti][0:n, :],
                                                                          func=AF.Copy), r=[K("ps", ti)], w=[yk])
                        self.act(lambda e, ti=ti, n=n, cb=cb: e.activation(
                            out=fjunk[0:n, :], in_=PS[ti][0:n, :], func=AF.Square,
                            accum_out=small[0:n, 48 + ti * 4 + cb:49 + ti * 4 + cb]),
                            r=[K("ps", ti)], w=[K("fjunk"), K("fss", ti, cb)])
                        yr0 = 128 * tt if tt < NPT else TP
                        self.dma("sp", yscr[yr0:yr0 + n, cb * 512:(cb + 1) * 512], ys[0:n, :], r=[yk],
                                 w=[K("yscr", tt)])
                for ti, (tt, lc, n) in enumerate(tiles):
                    xtile = fxt[tt % 2]
                    xk = K("fxt", tt % 2)
                    ytile = fyt[tt % 2]
                    yk = K("fyt", tt % 2)
                    yr0 = 128 * tt if tt < NPT else TP
                    self.dma("sp", xtile[0:n, :], xres_ap(tt), r=[xkey(tt)], w=[xk])
                    self.dma("sp", ytile[0:n, :], yscr[yr0:yr0 + n, :], r=[K("yscr", tt)], w=[yk])
                    self.dve(lambda e, ti=ti, n=n: e.tensor_reduce(out=small[0:n, 70:71],
                                                                  in_=small[0:n, 48 + ti * 4:52 + ti * 4], axis=X,
                                                                  op=ALU.add),
                             r=[K("fss", ti, c) for c in range(4)], w=[K("fss4")])
                    rstd_from_ss(small[0:n, 70:71], small[0:n, 71:72], D, [K("fss4")], [K("frstd")])
                    self.dve(lambda e, n=n, ytile=ytile: e.scalar_tensor_tensor(
                        out=ytile[0:n, :], in0=ytile[0:n, :], scalar=small[0:n, 71:72], in1=gpostF[0:n, :],
                        op0=ALU.mult, op1=ALU.mult), r=[yk, K("frstd"), K("gpostF")], w=[yk])
                    self.dve(lambda e, n=n, ytile=ytile, xtile=xtile: e.tensor_tensor(
                        out=xtile[0:n, :], in0=xtile[0:n, :], in1=ytile[0:n, :], op=ALU.add), r=[yk, xk], w=[xk])
                    if final:
                        self.dma("sp", yout_ap(tt), xtile[0:n, :], r=[xk], w=[K("yout", tt)])
                    else:
                        self.dma("sp", xres_ap(tt), xtile[0:n, :], r=[xk], w=[xkey(tt)])
            fence()

        S.emit(nc)
        return nc


def _host_consts(cfg, j):
    NT, NPT, TP, PAST = cfg.NT, cfg.NPT, cfg.TP, cfg.PAST
    half = 8
    inv_freq = np.power(np.float32(500000.0), -np.arange(half, dtype=np.float32) * np.float32(2.0 / 16)).astype(np.float32)
    cosd = np.zeros((128, NT, 64), np.float32)
    sind = np.zeros((128, NT, 64), np.float32)
    for tt in range(NT):
        if tt < NPT:
            pos = (j * TP + 128 * tt + np.arange(128)).astype(np.float32)
        else:
            pos = np.zeros(128, np.float32)
            pos[:TS] = (PAST + np.arange(TS)).astype(np.float32)
        ang = pos[:, None] * inv_freq[None, :]
        cosd[:, tt, :] = np.tile(np.cos(ang).astype(np.float32), (1, 8))
        sind[:, tt, :] = np.tile(np.sin(ang).astype(np.float32), (1, 8))
    cc = np.zeros((128, 12), np.float32)
    for s in range(3):
        cc[:, s] = 0.0 if s < j else NEG
    for r in range(4):
        cc[:, 3 + r] = 1.0 if r == j - 1 else 0.0
        cc[:, 7 + r] = 1.0 if r == j else 0.0
    masks = np.zeros((128, 4, 512), np.float32)
    k = np.arange(128)[:, None]
    q = np.arange(512)[None, :]
    for m in range(4):
        masks[:, m, :] = (((128 * m + k) // 64) <= (q // 64)).astype(np.float32)
    triu = np.triu(np.ones((128, 128), np.float32))
    ident = np.eye(128, dtype=np.float32)
    return cosd, sind, cc, masks, triu, ident


def _fm(v):
    n = v.shape[-1] // 128
    return np.swapaxes(v.reshape(v.shape[:-1] + (n, 128)), -1, -2)


def run(cfg, inputs, trace=False):
    L, TP, PAST = cfg.L, cfg.TP, cfg.PAST
    f32 = np.float32
    A = lambda k: np.ascontiguousarray(np.asarray(inputs[k], dtype=f32))
    x_prompt, x_sample = A("x_prompt"), A("x_sample")
    cache_k, cache_v = A("cache_k"), A("cache_v")
    st_h, st_c = A("state_lru_h"), A("state_conv")
    w_rg = np.ascontiguousarray(np.concatenate([A("w_rg_a"), A("w_rg_x")], axis=1))
    bvec = np.ascontiguousarray(np.concatenate([
        A("g_mix_post"), A("g_ffn_post"), A("g_mlp_v"), A("b_mlp_v"), A("b_spatial").reshape(L, 512),
        A("lam_q1"), A("lam_k1"), A("lam_q2"), A("lam_k2")], axis=1))
    common_pv = [
        _fm(A("g_mix_pre")), _fm(A("g_ffn_pre")),
        np.swapaxes(_fm(A("conv_w")), 1, 2).reshape(L, 128, 4, 4).transpose(0, 1, 3, 2).reshape(L, 128, 16),
        _fm(A("conv_b")), _fm(A("b_rg_a")), _fm(A("b_rg_x")), _fm(A("lru_lambda")),
        A("g_subln").reshape(L, 128, 1),
    ]
    common_pv[2] = _fm(A("conv_w")).transpose(0, 2, 3, 1).reshape(L, 128, 16)
    shared = {
        "w_in": A("w_in"), "w_out": A("w_out"), "w_gate": A("w_gate"), "w_up": A("w_up"), "w_down": A("w_down"),
        "w_rg": w_rg, "w_sp": A("w_spatial"), "bvec": bvec,
    }
    in_maps = []
    for c in range(8):
        b, j = c // 4, c % 4
        cosd, sind, cc, masks, triu, ident = _host_consts(cfg, j)
        pv = np.concatenate(common_pv + [
            _fm(st_h[:, c]),
            _fm(st_c[:, c]).transpose(0, 2, 3, 1).reshape(L, 128, 12),
        ], axis=2)
        m = dict(shared)
        m.update({
            "xp": np.ascontiguousarray(x_prompt[b, j * TP:(j + 1) * TP]),
            "xs": np.ascontiguousarray(x_sample[c]),
            "ck": np.ascontiguousarray(cache_k[:, c].reshape(L, PAST, 1024)),
            "cv": np.ascontiguousarray(cache_v[:, c].reshape(L, PAST, 1024)),
            "pvec": np.ascontiguousarray(pv.astype(f32)),
            "cosd": cosd, "sind": sind, "cconst": cc, "masks": masks, "triu": triu, "ident": ident,
        })
        in_maps.append(m)
    bld = Builder(cfg)
    nc = bld.build()
    res = run_bass_kernel_spmd(nc, in_maps, core_ids=list(range(8)), **({"trace": True} if trace else {}))
    R = res.results
    G = lambda c, k: np.asarray(R[c][k], dtype=f32)
    SEQ = 4 * TP
    y_prompt = np.stack([np.concatenate([G(b * 4 + j, "y_p") for j in range(4)], 0) for b in range(2)], 0)
    y_sample = np.stack([G(c, "y_s") for c in range(8)], 0)
    k_prompt = np.stack([np.concatenate([G(b * 4 + j, "k_p") for j in range(4)], 1) for b in range(2)], 1)
    v_prompt = np.stack([np.concatenate([G(b * 4 + j, "v_p") for j in range(4)], 1) for b in range(2)], 1)
    k_prompt = k_prompt.reshape(L, 2, SEQ, 8, 128)
    v_prompt = v_prompt.reshape(L, 2, SEQ, 8, 128)
    unfm = lambda a: np.swapaxes(a, -1, -2).reshape(a.shape[:-2] + (512,))
    h_prompt = np.stack([unfm(G(b * 4 + 3, "h_p")) for b in range(2)], 1)
    uc = lambda a: a.transpose(0, 3, 2, 1).reshape(L, 3, 512)
    conv_prompt = np.stack([uc(G(b * 4 + 3, "c_p")) for b in range(2)], 1)
    k_sample = np.stack([G(c, "k_s") for c in range(8)], 1).reshape(L, 8, TS, 8, 128)
    v_sample = np.stack([G(c, "v_s") for c in range(8)], 1).reshape(L, 8, TS, 8, 128)
    h_sample = np.stack([unfm(G(c, "h_s")) for c in range(8)], 1)
    conv_sample = np.stack([uc(G(c, "c_s")) for c in range(8)], 1)
    chunk_v = np.stack([G(c, "cv_s") for c in range(8)], 1)
    outs = (y_prompt, y_sample, k_prompt, v_prompt, h_prompt, conv_prompt, k_sample, v_sample, h_sample,
            conv_sample, chunk_v)
    return tuple(np.ascontiguousarray(o.astype(f32)) for o in outs), res


def kernel(**inputs):
    cfg = Cfg()
    outs, _ = run(cfg, inputs)
    return outs
```

```python
import math
import types
import numpy as np
import ml_dtypes
import concourse.bass as bass
import concourse.mybir as mybir
from concourse.bass_utils import run_bass_kernel_spmd

F32 = mybir.dt.float32
BF16 = mybir.dt.bfloat16
U8 = mybir.dt.uint8
ALU = mybir.AluOpType
AF = mybir.ActivationFunctionType

D = 2048
DT = 16
NH = 8
TS = 32
EPS = 1e-6
NEG = -30000.0


class Cfg:
    def __init__(self, L=4, TP=2048, PAST=4096, DFF=5632):
        self.L, self.TP, self.PAST, self.DFF = L, TP, PAST, DFF
        self.NPT = TP // 128
        self.NT = self.NPT + 1
        self.NTOK = TP + TS
        self.FT = DFF // 128
        self.NQB = TP // 512
        self.NKT = TP // 128
        self.PKT = PAST // 128


class Op:
    __slots__ = ("eng", "fn", "deps", "dma", "signal", "sigval", "sem", "target", "prev_target", "inc")


class Sched:
    ENGS = ("pe", "act", "dve", "pool", "sp")

    def __init__(self, n_dma_sems=40):
        self.ops = []
        self.last_w = {}
        self.readers = {}
        self.n_dma_sems = n_dma_sems
        self.dma_rr = 0
        self.dma_sem_val = [0] * n_dma_sems
        self.fence_op = None

    @staticmethod
    def _freeze(fn):
        if fn.__closure__ is None:
            return fn
        cells = []
        for c in fn.__closure__:
            try:
                cells.append(types.CellType(c.cell_contents))
            except ValueError:
                cells.append(c)
        return types.FunctionType(fn.__code__, fn.__globals__, fn.__name__, fn.__defaults__, tuple(cells))

    def op(self, eng, fn, reads=(), writes=(), dma=False, inc=16):
        fn = self._freeze(fn)
        o = Op()
        o.eng, o.fn, o.dma, o.signal, o.sigval, o.inc = eng, fn, dma, False, 0, inc
        idx = len(self.ops)
        deps = {}
        for k in reads:
            w = self.last_w.get(k)
            if w is not None:
                deps[w] = True
        for k in writes:
            w = self.last_w.get(k)
            if w is not None:
                deps[w] = True
            for r in self.readers.get(k, ()):
                if r not in deps:
                    deps[r] = False
        if self.fence_op is not None:
            deps[self.fence_op] = True
        o.deps = deps
        if dma:
            s = self.dma_rr % self.n_dma_sems
            self.dma_rr += 1
            o.sem = s
            o.prev_target = self.dma_sem_val[s]
            self.dma_sem_val[s] += inc
            o.target = self.dma_sem_val[s]
        self.ops.append(o)
        for k in reads:
            self.readers.setdefault(k, []).append(idx)
        for k in writes:
            self.last_w[k] = idx
            self.readers[k] = []
        for k in reads:
            lst = self.readers[k]
            if len(lst) > 12:
                keep = {}
                out = []
                for r in lst:
                    ro = self.ops[r]
                    if ro.dma:
                        out.append(r)
                    else:
                        keep[ro.eng] = r
                self.readers[k] = out + list(keep.values())
        return idx

    def emit(self, nc):
        ops = self.ops
        for o in ops:
            for d, hard in o.deps.items():
                p = ops[d]
                if p.dma:
                    continue
                if p.eng == o.eng and not o.dma:
                    if p.eng == "pe" or not hard:
                        continue
                p.signal = True
        cnt = {e: 0 for e in self.ENGS}
        for o in ops:
            if not o.dma and o.signal:
                cnt[o.eng] += 1
                o.sigval = cnt[o.eng]
        engobj = {"pe": nc.tensor, "act": nc.scalar, "dve": nc.vector, "pool": nc.gpsimd, "sp": nc.sync}
        import contextlib
        with contextlib.ExitStack() as st:
            esem = {e: st.enter_context(nc.semaphore("eng_" + e)) for e in self.ENGS}
            dsem = [st.enter_context(nc.semaphore("dma_%d" % i)) for i in range(self.n_dma_sems)]
            block = st.enter_context(nc.Block())

            def run(ename):
                def body(eng):
                    waited_e = {e: 0 for e in self.ENGS}
                    waited_d = {}
                    for o in ops:
                        if o.eng != ename:
                            continue
                        need_e = {}
                        need_d = {}
                        for d, hard in o.deps.items():
                            p = ops[d]
                            if p.dma:
                                if need_d.get(p.sem, 0) < p.target:
                                    need_d[p.sem] = p.target
                            else:
                                if p.eng == o.eng and not o.dma:
                                    if p.eng == "pe" or not hard:
                                        continue
                                if need_e.get(p.eng, 0) < p.sigval:
                                    need_e[p.eng] = p.sigval
                        if o.dma and o.prev_target > 0:
                            if need_d.get(o.sem, 0) < o.prev_target:
                                need_d[o.sem] = o.prev_target
                        for e, v in need_e.items():
                            if waited_e[e] < v:
                                eng.wait_ge(esem[e], v)
                                waited_e[e] = v
                        for s, v in need_d.items():
                            if waited_d.get(s, 0) < v:
                                eng.wait_ge(dsem[s], v)
                                waited_d[s] = v
                        ins = o.fn(eng)
                        if o.dma:
                            ins.then_inc(dsem[o.sem], o.inc)
                        elif o.signal:
                            ins.then_inc(esem[o.eng], 1)
                    if ename in ("sp", "pool"):
                        last = {}
                        for o in ops:
                            if o.dma and o.eng == ename:
                                last[o.sem] = max(last.get(o.sem, 0), o.target)
                        for s, v in last.items():
                            if waited_d.get(s, 0) < v:
                                eng.wait_ge(dsem[s], v)
                return body

            block.sync(run("sp"))
            block.gpsimd(run("pool"))
            block.scalar(run("act"))
            block.vector(run("dve"))
            block.tensor(run("pe"))


class Builder:
    def __init__(self, cfg):
        self.cfg = cfg
        self.nc = bass.Bass("TRN2", target_bir_lowering=False)
        self.S = Sched()
        self.arena_off = 0
        self.dram = {}

    def din(self, name, shape, dt=F32):
        t = self.nc.dram_tensor(name, list(shape), dt, kind="ExternalInput")
        self.dram[name] = t
        return t.ap()

    def dout(self, name, shape, dt=F32):
        t = self.nc.dram_tensor(name, list(shape), dt, kind="ExternalOutput")
        self.dram[name] = t
        return t.ap()

    def dscr(self, name, shape, dt):
        if getattr(self.cfg, "debug", False) and name in ("xres_p", "xres_s", "yscr", "qt_d", "xa_d", "lru_a", "lru_b", "dbg_ycat"):
            t = self.nc.dram_tensor(name, list(shape), dt, kind="ExternalOutput")
        else:
            t = self.nc.dram_tensor(name, list(shape), dt)
        self.dram[name] = t
        return t.ap()

    def salloc(self, shape, dt, off=None):
        sz = 4 if dt == F32 else 2
        n = int(np.prod(shape[1:]))
        nbytes = (n * sz + 63) // 64 * 64
        if off is None:
            off = self.arena_off
            self.arena_off += nbytes
        v = self.A[:, off:off + n * sz].bitcast(dt)
        if len(shape) == 3:
            v = v.rearrange("p (a b) -> p a b", a=shape[1])
        elif len(shape) == 4:
            v = v.rearrange("p (a b c) -> p a b c", a=shape[1], b=shape[2])
        return v

    def act(self, fn, r=(), w=()):
        return self.S.op("act", fn, r, w)

    def dve(self, fn, r=(), w=()):
        return self.S.op("dve", fn, r, w)

    def pool(self, fn, r=(), w=()):
        return self.S.op("pool", fn, r, w)

    def pe(self, fn, r=(), w=()):
        return self.S.op("pe", fn, r, w)

    def dma(self, q, out, in_, r=(), w=()):
        return self.S.op(q, lambda e: e.dma_start(out=out, in_=in_), r, w, dma=True, inc=16)

    def build(self):
        cfg, nc = self.cfg, self.nc
        L, TP, PAST, DFF, NT, NPT, NTOK, FT = cfg.L, cfg.TP, cfg.PAST, cfg.DFF, cfg.NT, cfg.NPT, cfg.NTOK, cfg.FT
        NCH = TP // 512
        NKT = TP // 128
        PKT = PAST // 128
        FQ = FT // 4
        S = self.S
        X = mybir.AxisListType.X
        xp = self.din("xp", [TP, D])
        xs = self.din("xs", [TS, D])
        ck = self.din("ck", [L, PAST, 1024])
        cv = self.din("cv", [L, PAST, 1024])
        w_in = self.din("w_in", [L, D, 5120])
        w_out = self.din("w_out", [L, D, D])
        w_gate = self.din("w_gate", [L, D, DFF])
        w_up = self.din("w_up", [L, D, DFF])
        w_down = self.din("w_down", [L, DFF, D])
        w_rg = self.din("w_rg", [L, 8, 128, 128])
        w_sp = self.din("w_sp", [L, 4, 128, 128])
        NPV = 81
        pvec = self.din("pvec", [L, 128, NPV])
        NBV = 5888
        bvec = self.din("bvec", [L, NBV])
        cosd = self.din("cosd", [128, NT, 64])
        sind = self.din("sind", [128, NT, 64])
        NCC = 12
        cconst = self.din("cconst", [128, NCC])
        masks = self.din("masks", [128, 4, 512])
        triu = self.din("triu", [128, 128])
        ident = self.din("ident", [128, 128])

        y_p = self.dout("y_p", [TP, D])
        y_s = self.dout("y_s", [TS, D])
        k_p = self.dout("k_p", [L, TP, 1024])
        v_p = self.dout("v_p", [L, TP, 1024])
        h_p = self.dout("h_p", [L, 128, 4])
        c_p = self.dout("c_p", [L, 128, 4, 3])
        k_s = self.dout("k_s", [L, TS, 1024])
        v_s = self.dout("v_s", [L, TS, 1024])
        h_s = self.dout("h_s", [L, 128, 4])
        c_s = self.dout("c_s", [L, 128, 4, 3])
        cv_s = self.dout("cv_s", [L, TS, 512])

        xres_p = self.dscr("xres_p", [TP, D], F32)
        xres_s = self.dscr("xres_s", [TS, D], F32)
        yscr = self.dscr("yscr", [TP + TS, D], F32)
        NVC = TP // 512
        KT_own = [self.dscr("kt_own%d" % i, [1024, 512], BF16) for i in range(NVC)]
        KT_all = [self.dscr("kt_all%d" % i, [4 * 1024, 512], BF16) for i in range(NVC)]
        V_own = [self.dscr("v_own%d" % i, [512, 1024], BF16) for i in range(NVC)]
        V_all = [self.dscr("v_all%d" % i, [4 * 512, 1024], BF16) for i in range(NVC)]
        QT_d = self.dscr("qt_d", [8, 128, NTOK], BF16)
        halo_in = self.dscr("halo_in", [128, 12], F32)
        halo_all = self.dscr("halo_all", [4 * 128, 12], F32)
        ab_in = self.dscr("ab_in", [128, 8], F32)
        ab_all = self.dscr("ab_all", [4 * 128, 8], F32)
        xa_d = self.dscr("xa_d", [4, 128, TP], F32)
        xa_sd = self.dscr("xa_sd", [4, 128, TS], F32)
        lru_a = self.dscr("lru_a", [4, 128, TP], F32)
        lru_b = self.dscr("lru_b", [4, 128, TP], F32)
        dbg_ycat = self.dscr("dbg_ycat", [128, 16, NTOK], BF16) if getattr(cfg, "debug", False) else None

        ARENA = 206 * 1024
        self.arena_t = nc.alloc_sbuf_tensor("arena", [128, ARENA], U8)
        self.A = self.arena_t.ap()
        sa = self.salloc
        identF = sa([128, 128], F32)
        identB = sa([128, 128], BF16)
        onesB = sa([128, 128], BF16)
        onesF = sa([128, 128], F32)
        triuB = sa([128, 128], BF16)
        maskB = sa([128, 4, 512], BF16)
        cosT = sa([128, NT, 64], F32)
        sinT = sa([128, NT, 64], F32)
        cc = sa([128, NCC], F32)
        pv = sa([128, NPV + 3], F32)
        pv2 = sa([128, 32], F32)
        lamw = sa([128, 256], F32)
        wrg = sa([128, 8, 128], BF16)
        wspT = sa([128, 4, 128], BF16)
        wsp_raw = sa([128, 4, 128], BF16)
        small = sa([128, 96], F32)
        ksT = sa([128, 8, TS], BF16)
        vsB = sa([128, 1024], BF16)
        base_nc = self.arena_off
        ycat = sa([128, 16, NTOK], BF16)
        base_ph = self.arena_off
        xnT = sa([128, 16, 544], BF16)
        wbuf = [sa([128, 16, 512], BF16) for _ in range(2)]
        xt = [sa([128, 2048], F32) for _ in range(2)]
        xnb = sa([128, 2048], BF16)
        zf = [sa([128, 512], F32) for _ in range(2)]
        rt = sa([128, 4, 64], F32)
        stg = [sa([128, 512], BF16) for _ in range(2)]
        gel = [sa([128, 512], F32) for _ in range(3)]
        uact = sa([128, 4, 544], BF16)
        vnb = sa([128, 5, 512], BF16)
        mlpv = sa([128, 1536], F32)
        end_A = self.arena_off
        assert end_A <= ARENA, ("phase A overflow", end_A)
        self.arena_off = base_ph
        xa = sa([128, 3 + TP + 3 + TS], F32)
        xc = sa([128, TP + TS], F32)
        xcb = sa([128, TP + TS], BF16)
        la = [sa([128, TP], F32) for _ in range(2)]
        las = sa([128, 4, 2, TS], F32)
        hbuf = sa([128, TP], F32)
        lr = [sa([128, 512], F32) for _ in range(6)]
        halo_sb = sa([128, 4, 12], F32)
        ab_sb = sa([128, 4, 8], F32)
        ab_st = sa([128, 8], F32)
        assert self.arena_off <= ARENA, ("lru overflow", self.arena_off)
        self.arena_off = base_ph
        KTh = [sa([128, 4 * TP], BF16) for _ in range(2)]
        Vh = [sa([128, 4 * NKT, 128], BF16) for _ in range(2)]
        QTh = [sa([128, NTOK], BF16) for _ in range(2)]
        Pb = [sa([128, 512], BF16) for _ in range(4)]
        fin = [sa([128, 512], F32) for _ in range(6)]
        kc = sa([128, PKT, 128], BF16)
        vc = sa([128, PKT + 1, 128], BF16)
        KTc = sa([128, PAST + TS], BF16)
        assert self.arena_off <= ARENA, ("attention overflow", self.arena_off)
        self.arena_off = base_ph
        wout_sb = sa([128, 16, 2048], BF16)
        wxt = [sa([128, 2048], F32) for _ in range(2)]
        gpostA = sa([128, 2048], F32)
        wtmp = [sa([128, 512], F32) for _ in range(4)]
        wjunk = sa([128, 512], BF16)
        assert self.arena_off <= ARENA, ("wout overflow", self.arena_off)
        self.arena_off = base_nc
        hnT = sa([128, 16, 544], BF16)
        ffT = sa([128, FT, 544], BF16)
        gub = [sa([128, 16, 256], BF16) for _ in range(4)]
        wdb = [sa([128, FQ, 512], BF16) for _ in range(2)]
        fxt = [sa([128, 2048], F32) for _ in range(2)]
        fyt = [sa([128, 2048], F32) for _ in range(2)]
        fxnb = sa([128, 2048], BF16)
        gpostF = sa([128, 2048], F32)
        ft_ = [sa([128, 512], F32) for _ in range(4)]
        ystg = [sa([128, 512], F32) for _ in range(2)]
        fjunk = sa([128, 512], BF16)
        assert self.arena_off <= ARENA, ("ffn overflow", self.arena_off)

        PS = [nc.alloc_psum_tensor("ps%d" % i, [128, 512], F32).ap() for i in range(8)]

        def psB(i):
            return PS[i].bitcast(BF16)

        self.fence_keys = {}

        def K(*a):
            self.fence_keys[a] = True
            return a

        def fence():
            keys = list(self.fence_keys.keys())
            self.S.fence_op = self.dve(lambda e: e.memset(small[:, 90:91], 0.0), r=keys, w=keys)

        self.dma("sp", identF, ident[:, :], w=[K("identF")])
        self.dma("pool", identB, ident[:, :], w=[K("identB")])
        self.dma("pool", triuB, triu[:, :], w=[K("triuB")])
        self.dma("pool", maskB, masks[:, :, :], w=[K("maskB")])
        self.dma("sp", cosT, cosd[:, :, :], w=[K("cosT")])
        self.dma("sp", sinT, sind[:, :, :], w=[K("sinT")])
        self.dma("sp", cc, cconst[:, :], w=[K("cc")])
        self.dve(lambda e: e.memset(onesB, 1.0), w=[K("onesB")])
        self.dve(lambda e: e.memset(onesF, 1.0), w=[K("onesF")])

        def tokn(tt):
            return 128 if tt < NPT else TS

        def tcol(tt):
            return 128 * tt

        def x_src(l, tt):
            if l == 0:
                return xp[128 * tt:128 * tt + 128, :] if tt < NPT else xs[:, :]
            return xres_p[128 * tt:128 * tt + 128, :] if tt < NPT else xres_s[:, :]

        def xres_ap(tt):
            return xres_p[128 * tt:128 * tt + 128, :] if tt < NPT else xres_s[:, :]

        def yout_ap(tt):
            return y_p[128 * tt:128 * tt + 128, :] if tt < NPT else y_s[:, :]

        def xkey(tt):
            return K("xres", tt)

        def chunk_tiles(ci):
            tl = [(4 * ci + i, 128 * i, 128) for i in range(4)]
            if ci == NCH - 1:
                tl.append((NPT, 512, TS))
            return tl

        def chunk_subs(ci):
            return [(0, 512)] + ([(512, TS)] if ci == NCH - 1 else [])

        def rstd_from_ss(ss_ap, out_ap, n, rk, wk):
            self.act(lambda e: e.activation(out=out_ap, in_=ss_ap, func=AF.Ln, scale=1.0 / n, bias=EPS), r=rk, w=wk)
            self.act(lambda e: e.activation(out=out_ap, in_=out_ap, func=AF.Exp, scale=-0.5), r=wk, w=wk)

        self.wb_n = 0

        def load_wblock(src_ap):
            s = self.wb_n % 2
            self.wb_n += 1
            key = K("wbuf", s)
            self.dma("pool", wbuf[s], src_ap.rearrange("(kt p) c -> p kt c", p=128), w=[key])
            return wbuf[s], key

        def norm_transpose(n, xtile, xk, dstT, dcol, dkey, g0, xnb_t, kpre):
            ssk = K(kpre + "ss")
            xnk = K(kpre + "xnb")
            self.act(lambda e: e.activation(out=xnb_t[0:n, :], in_=xtile[0:n, :], func=AF.Square,
                                            accum_out=small[0:n, 0:1]), r=[xk], w=[xnk, ssk])
            rstd_from_ss(small[0:n, 0:1], small[0:n, 1:2], D, [ssk], [K(kpre + "rstd")])
            self.dve(lambda e: e.tensor_scalar(out=xnb_t[0:n, :], in0=xtile[0:n, :], scalar1=small[0:n, 1:2],
                                               scalar2=None, op0=ALU.mult),
                     r=[xk, K(kpre + "rstd")], w=[xnk])
            for g4 in range(4):
                bank = 4 + g4
                bk = K("ps", bank)
                for i in range(4):
                    dt_ = g4 * 4 + i
                    self.pe(lambda e, dt_=dt_, i=i, bank=bank: e.transpose(
                        out=psB(bank)[:, i * 128:i * 128 + n], in_=xnb_t[0:n, dt_ * 128:(dt_ + 1) * 128],
                        identity=identB[0:n, 0:n]), r=[xnk, K("identB")], w=[bk])
                for i in range(4):
                    dt_ = g4 * 4 + i
                    if i % 2 == 0:
                        self.act(lambda e, dt_=dt_, i=i, bank=bank: e.activation(
                            out=dstT[:, dt_, dcol:dcol + n], in_=psB(bank)[:, i * 128:i * 128 + n], func=AF.Copy,
                            scale=pv[:, g0 + dt_:g0 + dt_ + 1]), r=[bk, K("pv")], w=[dkey])
                    else:
                        self.dve(lambda e, dt_=dt_, i=i, bank=bank: e.tensor_scalar(
                            out=dstT[:, dt_, dcol:dcol + n], in0=psB(bank)[:, i * 128:i * 128 + n],
                            scalar1=pv[:, g0 + dt_:g0 + dt_ + 1], scalar2=None, op0=ALU.mult),
                            r=[bk, K("pv")], w=[dkey])

        def gelu(src, dst, n, ncol, rk, wk):
            t0, t1 = gel[0], gel[1]
            k0, k1 = K("gel", 0), K("gel", 1)
            self.act(lambda e: e.activation(out=t0[0:n, 0:ncol], in_=src, func=AF.Square), r=rk, w=[k0])
            self.dve(lambda e: e.tensor_scalar(out=t0[0:n, 0:ncol], in0=t0[0:n, 0:ncol], scalar1=0.044715,
                                               scalar2=1.0, op0=ALU.mult, op1=ALU.add), r=[k0], w=[k0])
            self.dve(lambda e: e.tensor_tensor(out=t0[0:n, 0:ncol], in0=src, in1=t0[0:n, 0:ncol], op=ALU.mult),
                     r=rk + [k0], w=[k0])
            self.act(lambda e: e.activation(out=t1[0:n, 0:ncol], in_=t0[0:n, 0:ncol], func=AF.Exp,
                                            scale=-1.5957691216), r=[k0], w=[k1])
            self.act(lambda e: e.activation(out=t1[0:n, 0:ncol], in_=t1[0:n, 0:ncol], func=AF.Ln, scale=1.0,
                                            bias=1.0), r=[k1], w=[k1])
            self.act(lambda e: e.activation(out=t1[0:n, 0:ncol], in_=t1[0:n, 0:ncol], func=AF.Exp, scale=-1.0),
                     r=[k1], w=[k1])
            self.dve(lambda e: e.tensor_tensor(out=dst, in0=src, in1=t1[0:n, 0:ncol], op=ALU.mult),
                     r=rk + [k1], w=wk)

        RG = [[0, 1, 2, 3], [4, 5, 6, 7]]

        def allgather(src_ap, dst_ap, rk, wk):
            S.op("pool", lambda e: e.collective_compute("AllGather", ALU.bypass, replica_groups=RG,
                                                        ins=[src_ap.opt()], outs=[dst_ap.opt()]),
                 rk, wk, dma=True, inc=1)

        bankrr = [0]

        def nb():
            b = bankrr[0] % 4
            bankrr[0] += 1
            return b

        for l in range(L):
            lam_init = 0.8 - 0.6 * math.exp(-0.3 * l)
            final = (l == L - 1)
            self.dma("sp", pv[:, 0:NPV], pvec[l, :, :], w=[K("pv")])
            self.dma("sp", lamw, bvec[l:l + 1, 5632:5888].partition_broadcast(128), w=[K("lamw")])
            self.dma("sp", mlpv, bvec[l:l + 1, 4096:5632].partition_broadcast(128), w=[K("mlpv")])
            self.dma("pool", wrg, w_rg[l].rearrange("a c d -> c a d"), w=[K("wrg")])
            self.dma("pool", wsp_raw, w_sp[l].rearrange("g p q -> p g q"), w=[K("wsp_raw")])
            self.dve(lambda e: e.tensor_scalar(out=pv2[:, 0:8], in0=pv[:, 52:60], scalar1=-1.0, scalar2=None,
                                               op0=ALU.mult), r=[K("pv")], w=[K("pv2a")])
            self.act(lambda e: e.activation(out=pv2[:, 16:20], in_=pv[:, 60:64], func=AF.Exp, scale=-1.0),
                     r=[K("pv")], w=[K("pv2t")])
            self.act(lambda e: e.activation(out=pv2[:, 16:20], in_=pv2[:, 16:20], func=AF.Ln, scale=1.0, bias=1.0),
                     r=[K("pv2t")], w=[K("pv2t")])
            self.dve(lambda e: e.tensor_scalar(out=pv2[:, 8:12], in0=pv2[:, 16:20], scalar1=-8.0, scalar2=None,
                                               op0=ALU.mult), r=[K("pv2t")], w=[K("pv2cl")])
            self.dve(lambda e: e.tensor_scalar(out=pv2[:, 12:13], in0=pv[:, 64:65], scalar1=float(1.0 - lam_init),
                                               scalar2=None, op0=ALU.mult), r=[K("pv")], w=[K("pv2gs")])
            self.dve(lambda e: e.tensor_tensor(out=lamw[:, 0:64], in0=lamw[:, 0:64], in1=lamw[:, 64:128], op=ALU.mult),
                     r=[K("lamw")], w=[K("lamw")])
            self.dve(lambda e: e.tensor_tensor(out=lamw[:, 128:192], in0=lamw[:, 128:192], in1=lamw[:, 192:256],
                                               op=ALU.mult), r=[K("lamw")], w=[K("lamw")])
            self.dve(lambda e: e.tensor_reduce(out=pv2[:, 20:21], in_=lamw[:, 0:64], axis=X, op=ALU.add),
                     r=[K("lamw")], w=[K("pv2l")])
            self.dve(lambda e: e.tensor_reduce(out=pv2[:, 21:22], in_=lamw[:, 128:192], axis=X, op=ALU.add),
                     r=[K("lamw")], w=[K("pv2l")])
            self.act(lambda e: e.activation(out=pv2[:, 20:22], in_=pv2[:, 20:22], func=AF.Exp), r=[K("pv2l")],
                     w=[K("pv2l")])
            self.dve(lambda e: e.scalar_tensor_tensor(out=pv2[:, 13:14], in0=pv2[:, 21:22], scalar=float(-lam_init),
                                                      in1=pv2[:, 20:21], op0=ALU.add, op1=ALU.subtract),
                     r=[K("pv2l")], w=[K("pv2nl")])
            for g in range(4):
                self.pe(lambda e, g=g: e.transpose(out=psB(4)[:, g * 128:(g + 1) * 128], in_=wsp_raw[:, g, :],
                                                   identity=identB), r=[K("wsp_raw"), K("identB")], w=[K("ps", 4)])
            for g in range(4):
                self.dve(lambda e, g=g: e.tensor_tensor(out=wspT[:, g, :], in0=psB(4)[:, g * 128:(g + 1) * 128],
                                                        in1=triuB, op=ALU.mult),
                         r=[K("ps", 4), K("triuB")], w=[K("wspT")])

            def proj_T(wb, wkey, lc, n, bank):
                for dt_ in range(DT):
                    self.pe(lambda e, dt_=dt_: e.matmul(PS[bank][0:n, :], lhsT=xnT[:, dt_, lc:lc + n],
                                                        rhs=wb[:, dt_, :], start=(dt_ == 0), stop=(dt_ == DT - 1)),
                            r=[K("xnT"), wkey], w=[K("ps", bank)])

            def proj_F(wb, wkey, ct, c0, ncol, bank):
                for dt_ in range(DT):
                    self.pe(lambda e, dt_=dt_: e.matmul(PS[bank][:, 0:ncol], lhsT=wb[:, dt_, ct * 128:(ct + 1) * 128],
                                                        rhs=xnT[:, dt_, c0:c0 + ncol], start=(dt_ == 0),
                                                        stop=(dt_ == DT - 1)),
                            r=[K("xnT"), wkey], w=[K("ps", bank)])

            def rope(zt, zk, n, tt):
                v = zt[0:n, :].rearrange("p (g e) -> p g e", g=8)
                x1, x2 = v[:, :, 0:8], v[:, :, 8:16]
                cs = cosT[0:n, tt, :].rearrange("p (g e) -> p g e", g=8)
                sn = sinT[0:n, tt, :].rearrange("p (g e) -> p g e", g=8)
                t = [rt[0:n, i, :].rearrange("p (g e) -> p g e", g=8) for i in range(4)]
                self.dve(lambda e: e.tensor_tensor(out=t[0], in0=x1, in1=cs, op=ALU.mult), r=[zk, K("cosT")],
                         w=[K("rt", 0)])
                self.dve(lambda e: e.tensor_tensor(out=t[1], in0=x2, in1=sn, op=ALU.mult), r=[zk, K("sinT")],
                         w=[K("rt", 1)])
                self.dve(lambda e: e.tensor_tensor(out=t[2], in0=x2, in1=cs, op=ALU.mult), r=[zk, K("cosT")],
                         w=[K("rt", 2)])
                self.dve(lambda e: e.tensor_tensor(out=t[3], in0=x1, in1=sn, op=ALU.mult), r=[zk, K("sinT")],
                         w=[K("rt", 3)])
                self.dve(lambda e: e.tensor_tensor(out=x1, in0=t[0], in1=t[1], op=ALU.subtract),
                         r=[K("rt", 0), K("rt", 1), zk], w=[zk])
                self.dve(lambda e: e.tensor_tensor(out=x2, in0=t[2], in1=t[3], op=ALU.add),
                         r=[K("rt", 2), K("rt", 3), zk], w=[zk])

            zi = [0]
            for ci in range(NCH):
                tiles = chunk_tiles(ci)
                subs = chunk_subs(ci)
                gc0 = 512 * ci
                for (tt, lc, n) in tiles:
                    xtile = xt[tt % 2]
                    xk = K("xt", tt % 2)
                    self.dma("sp", xtile[0:n, :], x_src(l, tt), r=[xkey(tt)], w=[xk])
                    norm_transpose(n, xtile, xk, xnT, lc, K("xnT"), 0, xnb, "A")

                def gcol(c0):
                    return gc0 + c0 if c0 < 512 else TP

                wb, wkey = load_wblock(w_in[l, :, 0:512])
                for ct in range(4):
                    for (c0, ncol) in subs:
                        bank = nb()
                        proj_F(wb, wkey, ct, c0, ncol, bank)
                        zt = zf[zi[0] % 2]
                        zk = K("zf", zi[0] % 2)
                        zi[0] += 1
                        self.act(lambda e, zt=zt, ncol=ncol, bank=bank: e.activation(
                            out=zt[:, 0:ncol], in_=PS[bank][:, 0:ncol], func=AF.Copy), r=[K("ps", bank)], w=[zk])
                        if c0 < 512:
                            self.dma("sp", xa_d[ct, :, gc0:gc0 + 512], zt[:, 0:512], r=[zk], w=[K("xa_d", ct)])
                        else:
                            self.dma("sp", xa_sd[ct, :, :], zt[:, 0:TS], r=[zk], w=[K("xa_sd", ct)])
                if ci == NCH - 1:
                    xadk = [K("xa_d", c) for c in range(4)]
                    self.dma("sp", halo_in[:, :].rearrange("p (c j) -> p c j", c=4),
                             xa_d[:, :, TP - 3:TP].rearrange("c p j -> p c j"), r=xadk, w=[K("halo_in")])
                    allgather(halo_in, halo_all, [K("halo_in")], [K("halo_all")])
                    self.dma("sp", c_p[l], xa_d[:, :, TP - 3:TP].rearrange("c p j -> p c j"), r=xadk, w=[K("c_p")])
                    self.dma("sp", c_s[l], xa_sd[:, :, TS - 3:TS].rearrange("c p j -> p c j"),
                             r=[K("xa_sd", c) for c in range(4)], w=[K("c_s")])
                wb, wkey = load_wblock(w_in[l, :, 512:1024])
                for ct in range(4):
                    for (c0, ncol) in subs:
                        bank = nb()
                        proj_F(wb, wkey, ct, c0, ncol, bank)
                        g0_ = gcol(c0)
                        gelu(PS[bank][:, 0:ncol], ycat[:, ct, g0_:g0_ + ncol], 128, ncol, [K("ps", bank)],
                             [K("ycat", "a", ct)])
                wb, wkey = load_wblock(w_in[l, :, 4096:4608])
                for ct in range(4):
                    for (c0, ncol) in subs:
                        bank = nb()
                        proj_F(wb, wkey, ct, c0, ncol, bank)
                        gelu(PS[bank][:, 0:ncol], uact[:, ct, c0:c0 + ncol], 128, ncol, [K("ps", bank)],
                             [K("uact")])
                pend_vc = [None]
                wb, wkey = load_wblock(w_in[l, :, 4608:5120])
                for ti, (tt, lc, n) in enumerate(tiles):
                    bank = nb()
                    proj_T(wb, wkey, lc, n, bank)
                    if pend_vc[0] is not None:
                        pend_vc[0]()

                    def post_vc(ti=ti, tt=tt, lc=lc, n=n, bank=bank):
                        g2 = gel[2]
                        gelu(PS[bank][0:n, :], g2[0:n, :], n, 512, [K("ps", bank)], [K("gel2")])
                        self.dve(lambda e, n=n: e.bn_stats(out=small[0:n, 8:14], in_=g2[0:n, :]), r=[K("gel2")],
                                 w=[K("bn")])
                        self.dve(lambda e, n=n: e.bn_aggr(out=small[0:n, 14:16], in_=small[0:n, 8:14]), r=[K("bn")],
                                 w=[K("bn2")])
                        self.act(lambda e, n=n: e.activation(out=small[0:n, 16:17], in_=small[0:n, 15:16], func=AF.Ln,
                                                             scale=1.0, bias=EPS), r=[K("bn2")], w=[K("bn3")])
                        self.act(lambda e, n=n: e.activation(out=small[0:n, 16:17], in_=small[0:n, 16:17], func=AF.Exp,
                                                             scale=-0.5), r=[K("bn3")], w=[K("bn3")])
                        self.dve(lambda e, n=n: e.tensor_scalar(out=g2[0:n, :], in0=g2[0:n, :], scalar1=small[0:n, 14:15],
                                                                scalar2=small[0:n, 16:17], op0=ALU.subtract,
                                                                op1=ALU.mult),
                                 r=[K("gel2"), K("bn2"), K("bn3")], w=[K("gel2")])
                        self.dve(lambda e, n=n: e.tensor_tensor(out=g2[0:n, :], in0=g2[0:n, :], in1=mlpv[0:n, 0:512],
                                                                op=ALU.mult), r=[K("gel2"), K("mlpv")], w=[K("gel2")])
                        self.dve(lambda e, n=n: e.tensor_tensor(out=g2[0:n, :], in0=g2[0:n, :], in1=mlpv[0:n, 512:1024],
                                                                op=ALU.add), r=[K("gel2"), K("mlpv")], w=[K("gel2")])
                        if tt == NPT:
                            self.dma("sp", cv_s[l], g2[0:n, :], r=[K("gel2")], w=[K("cv_s")])
                        self.dve(lambda e, n=n, ti=ti: e.tensor_copy(out=vnb[0:n, ti, :], in_=g2[0:n, :]), r=[K("gel2")],
                                 w=[K("vnb", ti)])
                        bank2 = nb()
                        for g in range(4):
                            self.pe(lambda e, g=g, n=n, ti=ti, bank2=bank2: e.matmul(
                                PS[bank2][:, g * 128:g * 128 + n], lhsT=vnb[0:n, ti, g * 128:(g + 1) * 128],
                                rhs=wspT[0:n, g, 0:n], start=True, stop=True),
                                r=[K("vnb", ti), K("wspT")], w=[K("ps", bank2)])
                        t0 = gel[0]
                        g0_ = gcol(lc)
                        self.dve(lambda e, n=n, bank2=bank2, t0=t0: e.tensor_tensor(
                            out=t0.rearrange("p (g q) -> p g q", g=4)[:, :, 0:n],
                            in0=PS[bank2].rearrange("p (g q) -> p g q", g=4)[:, :, 0:n],
                            in1=mlpv[:, 1024:1536].rearrange("p (g q) -> p g q", g=4)[:, :, 0:n], op=ALU.add),
                            r=[K("ps", bank2), K("mlpv")], w=[K("gel", 0)])
                        self.dve(lambda e, n=n, lc=lc, g0_=g0_, t0=t0: e.tensor_tensor(
                            out=ycat[:, 12:16, g0_:g0_ + n], in0=t0.rearrange("p (g q) -> p g q", g=4)[:, :, 0:n],
                            in1=uact[:, :, lc:lc + n], op=ALU.mult),
                            r=[K("gel", 0), K("uact")], w=[K("ycat", "c")])
                    pend_vc[0] = post_vc
                pend_vc[0]()
                pend_vc[0] = None
                pend_post = [None]
                for blk in range(6):
                    col0 = 1024 + blk * 512
                    wb, wkey = load_wblock(w_in[l, :, col0:col0 + 512])
                    kind = blk // 2
                    hb = (blk % 2) * 4
                    for (tt, lc, n) in tiles:
                        bank = nb()
                        proj_T(wb, wkey, lc, n, bank)
                        if pend_post[0] is not None:
                            pend_post[0]()

                        def post(tt=tt, lc=lc, n=n, bank=bank):
                            zt = zf[zi[0] % 2]
                            zk = K("zf", zi[0] % 2)
                            zi[0] += 1
                            self.act(lambda e, zt=zt, n=n, bank=bank: e.activation(out=zt[0:n, :], in_=PS[bank][0:n, :],
                                                                                 func=AF.Copy), r=[K("ps", bank)], w=[zk])
                            if kind < 2:
                                rope(zt, zk, n, tt)
                            if kind >= 1:
                                dst = (k_p if kind == 1 else v_p) if tt < NPT else (k_s if kind == 1 else v_s)
                                r0, r1 = (128 * tt, 128 * tt + 128) if tt < NPT else (0, TS)
                                self.dma("sp", dst[l, r0:r1, hb * 128:hb * 128 + 512], zt[0:n, :], r=[zk],
                                         w=[K("kvout", kind, blk % 2, tt)])
                            if kind < 2:
                                tb_ = 4 + (zi[0] % 4)
                                for hh in range(4):
                                    self.pe(lambda e, zt=zt, n=n, hh=hh, tb_=tb_: e.transpose(
                                        out=PS[tb_][:, hh * 128:hh * 128 + n], in_=zt[0:n, hh * 128:(hh + 1) * 128],
                                        identity=identF[0:n, 0:n]), r=[zk, K("identF")], w=[K("ps", tb_)])
                                psv = PS[tb_].rearrange("p (h t) -> p h t", h=4)[:, :, 0:n]
                                if kind == 1 and tt == NPT:
                                    self.act(lambda e, psv=psv, hb=hb: e.activation(out=ksT[:, hb:hb + 4, :], in_=psv,
                                                                                    func=AF.Copy),
                                             r=[K("ps", tb_)], w=[K("ksT")])
                                else:
                                    st_ = stg[zi[0] % 2]
                                    sk = K("stg", zi[0] % 2)
                                    stv = st_.rearrange("p (h t) -> p h t", h=4)[:, :, 0:n]
                                    self.act(lambda e, psv=psv, stv=stv: e.activation(out=stv, in_=psv, func=AF.Copy),
                                             r=[K("ps", tb_)], w=[sk])
                                    if kind == 0:
                                        g0_ = gcol(lc)
                                        self.dma("sp", QT_d[hb:hb + 4, :, g0_:g0_ + n].rearrange("h e t -> e h t"), stv,
                                                 r=[sk], w=[K("QT_d", hb // 4)])
                                    else:
                                        li = tt % 4
                                        self.dma("sp", KT_own[ci][hb * 128:(hb + 4) * 128,
                                                                  li * 128:(li + 1) * 128].rearrange("(h e) t -> e h t", h=4),
                                                 stv, r=[sk], w=[K("KT_own", ci)])
                            else:
                                if tt < NPT:
                                    st_ = stg[zi[0] % 2]
                                    sk = K("stg", zi[0] % 2)
                                    self.dve(lambda e, zt=zt, st_=st_: e.tensor_copy(out=st_, in_=zt), r=[zk], w=[sk])
                                    vd = V_own[tt // 4]
                                    r0 = (tt % 4) * 128
                                    self.dma("sp", vd[r0:r0 + 128, hb * 128:hb * 128 + 512], st_, r=[sk],
                                             w=[K("V_own", tt // 4)])
                                else:
                                    self.dve(lambda e, zt=zt, n=n, hb=hb: e.tensor_copy(
                                        out=vsB[0:n, hb * 128:hb * 128 + 512], in_=zt[0:n, :]), r=[zk], w=[K("vsB")])
                        pend_post[0] = post
                    pend_post[0]()
                    pend_post[0] = None
                    if blk == 3:
                        allgather(KT_own[ci], KT_all[ci], [K("KT_own", ci)], [K("KT_all", ci)])
                    if blk == 5:
                        allgather(V_own[ci], V_all[ci], [K("V_own", ci)], [K("V_all", ci)])

            fence()
            self.dma("sp", halo_sb, halo_all.rearrange("(r p) c -> p r c", p=128), r=[K("halo_all")],
                     w=[K("halo_sb")])
            fchunks = [(c, 512) for c in range(0, TP, 512)] + [(TP, TS)]
            SO = 3 + TP
            for ct in range(4):
                xak = K("xa")
                self.dma("sp", xa[:, 3:3 + TP], xa_d[ct], r=[K("xa_d", ct)], w=[xak])
                self.dma("sp", xa[:, SO + 3:SO + 3 + TS], xa_sd[ct], r=[K("xa_sd", ct)], w=[xak])
                self.dve(lambda e, ct=ct: e.tensor_scalar(out=xa[:, 0:3], in0=halo_sb[:, 0, ct * 3:ct * 3 + 3],
                                                          scalar1=cc[:, 3:4], scalar2=None, op0=ALU.mult),
                         r=[K("halo_sb"), K("cc"), xak], w=[xak])
                for r_ in range(1, 4):
                    self.dve(lambda e, r_=r_, ct=ct: e.scalar_tensor_tensor(
                        out=xa[:, 0:3], in0=halo_sb[:, r_, ct * 3:ct * 3 + 3], scalar=cc[:, 3 + r_:4 + r_],
                        in1=xa[:, 0:3], op0=ALU.mult, op1=ALU.add), r=[K("halo_sb"), K("cc"), xak], w=[xak])
                self.dve(lambda e, ct=ct: e.tensor_copy(out=xa[:, SO:SO + 3], in_=pv[:, 69 + ct * 3:72 + ct * 3]),
                         r=[K("pv"), xak], w=[xak])
                for (ro, T, xo) in ((0, TP, 0), (SO, TS, TP)):
                    self.dve(lambda e, ro=ro, T=T, xo=xo, ct=ct: e.tensor_scalar(
                        out=xc[:, xo:xo + T], in0=xa[:, ro:ro + T], scalar1=pv[:, 32 + ct * 4:33 + ct * 4],
                        scalar2=pv[:, 48 + ct:49 + ct], op0=ALU.mult, op1=ALU.add),
                        r=[xak, K("pv")], w=[K("xc")])
                    for j in range(1, 4):
                        self.dve(lambda e, ro=ro, T=T, xo=xo, ct=ct, j=j: e.scalar_tensor_tensor(
                            out=xc[:, xo:xo + T], in0=xa[:, ro + j:ro + j + T],
                            scalar=pv[:, 32 + ct * 4 + j:33 + ct * 4 + j], in1=xc[:, xo:xo + T],
                            op0=ALU.mult, op1=ALU.add), r=[xak, K("pv"), K("xc")], w=[K("xc")])
                self.dve(lambda e: e.tensor_copy(out=xcb, in_=xc), r=[K("xc")], w=[K("xcb")])
                for (c0, ncol) in fchunks:
                    ba, bx = nb(), nb()
                    self.pe(lambda e, ct=ct, c0=c0, ncol=ncol, ba=ba: e.matmul(
                        PS[ba][:, 0:ncol], lhsT=wrg[:, ct, :], rhs=xcb[:, c0:c0 + ncol], start=True, stop=True),
                        r=[K("wrg"), K("xcb")], w=[K("ps", ba)])
                    self.pe(lambda e, ct=ct, c0=c0, ncol=ncol, bx=bx: e.matmul(
                        PS[bx][:, 0:ncol], lhsT=wrg[:, 4 + ct, :], rhs=xcb[:, c0:c0 + ncol], start=True, stop=True),
                        r=[K("wrg"), K("xcb")], w=[K("ps", bx)])
                    er, ei, av, a2, sq, tb = [t[:, 0:ncol] for t in lr]
                    self.act(lambda e, ba=ba, ncol=ncol, ct=ct, er=er: e.activation(
                        out=er, in_=PS[ba][:, 0:ncol], func=AF.Exp, scale=-1.0, bias=pv2[:, ct:ct + 1]),
                        r=[K("ps", ba), K("pv2a")], w=[K("lr", 0)])
                    self.act(lambda e, bx=bx, ncol=ncol, ct=ct, ei=ei: e.activation(
                        out=ei, in_=PS[bx][:, 0:ncol], func=AF.Exp, scale=-1.0, bias=pv2[:, 4 + ct:5 + ct]),
                        r=[K("ps", bx), K("pv2a")], w=[K("lr", 1)])
                    self.act(lambda e, er=er: e.activation(out=er, in_=er, func=AF.Ln, scale=1.0, bias=1.0),
                             r=[K("lr", 0)], w=[K("lr", 0)])
                    self.act(lambda e, ei=ei: e.activation(out=ei, in_=ei, func=AF.Ln, scale=1.0, bias=1.0),
                             r=[K("lr", 1)], w=[K("lr", 1)])
                    self.act(lambda e, er=er: e.activation(out=er, in_=er, func=AF.Exp, scale=-1.0),
                             r=[K("lr", 0)], w=[K("lr", 0)])
                    self.act(lambda e, ei=ei: e.activation(out=ei, in_=ei, func=AF.Exp, scale=-1.0),
                             r=[K("lr", 1)], w=[K("lr", 1)])
                    if c0 < TP:
                        a_dst = la[0][:, c0:c0 + ncol]
                        b_dst = la[1][:, c0:c0 + ncol]
                        ak = [K("la")]
                    else:
                        a_dst = las[:, ct, 0, :]
                        b_dst = las[:, ct, 1, :]
                        ak = [K("las", ct)]
                    self.dve(lambda e, av=av, er=er, ct=ct: e.tensor_scalar(
                        out=av, in0=er, scalar1=pv2[:, 8 + ct:9 + ct], scalar2=None, op0=ALU.mult),
                        r=[K("lr", 0), K("pv2cl")], w=[K("lr", 2)])
                    self.act(lambda e, av=av, a_dst=a_dst: e.activation(out=a_dst, in_=av, func=AF.Exp),
                             r=[K("lr", 2)], w=ak)
                    self.act(lambda e, av=av, a2=a2: e.activation(out=a2, in_=av, func=AF.Exp, scale=2.0),
                             r=[K("lr", 2)], w=[K("lr", 3)])
                    self.act(lambda e, a2=a2, sq=sq: e.activation(out=sq, in_=a2, func=AF.Ln, scale=-1.0, bias=1.0),
                             r=[K("lr", 3)], w=[K("lr", 4)])
                    self.act(lambda e, sq=sq: e.activation(out=sq, in_=sq, func=AF.Exp, scale=0.5),
                             r=[K("lr", 4)], w=[K("lr", 4)])
                    self.dve(lambda e, sq=sq, ei=ei, tb=tb: e.tensor_tensor(out=tb, in0=sq, in1=ei, op=ALU.mult),
                             r=[K("lr", 4), K("lr", 1)], w=[K("lr", 5)])
                    self.dve(lambda e, tb=tb, c0=c0, ncol=ncol, b_dst=b_dst: e.tensor_tensor(
                        out=b_dst, in0=tb, in1=xc[:, c0:c0 + ncol], op=ALU.mult),
                        r=[K("lr", 5), K("xc")], w=ak)
                self.dve(lambda e: e.tensor_tensor_scan(out=hbuf, data0=la[0], data1=la[1], initial=0.0,
                                                        op0=ALU.mult, op1=ALU.add), r=[K("la")], w=[K("hbuf")])
                self.dve(lambda e, ct=ct: e.tensor_copy(out=ab_st[:, 4 + ct:5 + ct], in_=hbuf[:, TP - 1:TP]),
                         r=[K("hbuf")], w=[K("ab_st")])
                self.dve(lambda e: e.memset(xc[:, 0:TP], 0.0), r=[K("xc")], w=[K("xc")])
                self.dve(lambda e: e.tensor_tensor_scan(out=hbuf, data0=la[0], data1=xc[:, 0:TP], initial=1.0,
                                                        op0=ALU.mult, op1=ALU.add), r=[K("la"), K("xc")],
                         w=[K("hbuf")])
                self.dve(lambda e, ct=ct: e.tensor_copy(out=ab_st[:, ct:ct + 1], in_=hbuf[:, TP - 1:TP]),
                         r=[K("hbuf")], w=[K("ab_st")])
                self.dma("sp", lru_a[ct], la[0], r=[K("la")], w=[K("lru_a", ct)])
                self.dma("sp", lru_b[ct], la[1], r=[K("la")], w=[K("lru_b", ct)])
            self.dma("sp", ab_in[:, :], ab_st, r=[K("ab_st")], w=[K("ab_in")])
            allgather(ab_in, ab_all, [K("ab_in")], [K("ab_all")])
            self.dma("sp", ab_sb, ab_all.rearrange("(r p) c -> p r c", p=128), r=[K("ab_all")], w=[K("ab_sb")])
            hin = small[:, 24:28]
            Hs = small[:, 28:32]
            self.dve(lambda e: e.memset(small[:, 24:32], 0.0), w=[K("hin")])
            for s_ in range(3):
                self.dve(lambda e, s_=s_: e.tensor_tensor(out=Hs, in0=Hs, in1=ab_sb[:, s_, 0:4], op=ALU.mult),
                         r=[K("hin"), K("ab_sb")], w=[K("hin")])
                self.dve(lambda e, s_=s_: e.tensor_tensor(out=Hs, in0=Hs, in1=ab_sb[:, s_, 4:8], op=ALU.add),
                         r=[K("hin"), K("ab_sb")], w=[K("hin")])
                self.dve(lambda e, s_=s_: e.scalar_tensor_tensor(out=hin, in0=Hs, scalar=cc[:, 8 + s_:9 + s_],
                                                                 in1=hin, op0=ALU.mult, op1=ALU.add),
                         r=[K("hin"), K("cc")], w=[K("hin")])
            for ct in range(4):
                self.dma("sp", la[0], lru_a[ct], r=[K("lru_a", ct)], w=[K("la")])
                self.dma("sp", la[1], lru_b[ct], r=[K("lru_b", ct)], w=[K("la")])
                self.dve(lambda e, ct=ct: e.tensor_tensor_scan(out=hbuf, data0=la[0], data1=la[1],
                                                               initial=small[:, 24 + ct:25 + ct],
                                                               op0=ALU.mult, op1=ALU.add),
                         r=[K("la"), K("hin")], w=[K("hbuf")])
                self.dve(lambda e, ct=ct: e.tensor_copy(out=small[:, 32 + ct:33 + ct], in_=hbuf[:, TP - 1:TP]),
                         r=[K("hbuf")], w=[K("hlast")])
                self.dve(lambda e, ct=ct: e.tensor_tensor(out=ycat[:, ct, 0:TP], in0=hbuf, in1=ycat[:, ct, 0:TP],
                                                          op=ALU.mult), r=[K("hbuf"), K("ycat", "a", ct)],
                         w=[K("ycat", "a", ct)])
                self.dve(lambda e, ct=ct: e.tensor_tensor_scan(out=hbuf[:, 0:TS], data0=las[:, ct, 0, :],
                                                               data1=las[:, ct, 1, :], initial=pv[:, 65 + ct:66 + ct],
                                                               op0=ALU.mult, op1=ALU.add),
                         r=[K("las", ct), K("pv"), K("hbuf")], w=[K("hbuf")])
                self.dve(lambda e, ct=ct: e.tensor_copy(out=small[:, 36 + ct:37 + ct], in_=hbuf[:, TS - 1:TS]),
                         r=[K("hbuf")], w=[K("hlast")])
                self.dve(lambda e, ct=ct: e.tensor_tensor(out=ycat[:, ct, TP:TP + TS], in0=hbuf[:, 0:TS],
                                                          in1=ycat[:, ct, TP:TP + TS], op=ALU.mult),
                         r=[K("hbuf"), K("ycat", "a", ct)], w=[K("ycat", "a", ct)])
            self.dma("sp", h_p[l], small[:, 32:36], r=[K("hlast")], w=[K("h_p")])
            self.dma("sp", h_s[l], small[:, 36:40], r=[K("hlast")], w=[K("h_s")])

            fence()

            def attend(h, qT, qk, q0, nq, ktiles, out_col, rkeys, pre_hook=None):
                NKTL = len(ktiles)

                def s_mm(i):
                    kT, v_, nk, bias, m = ktiles[i]
                    sb = 2 * (i % 2)
                    self.pe(lambda e: e.matmul(PS[sb][0:nk, 0:nq], lhsT=kT[0:64, :], rhs=qT[0:64, q0:q0 + nq],
                                               start=True, stop=True), r=rkeys + [qk], w=[K("ps", sb)])
                    self.pe(lambda e: e.matmul(PS[sb + 1][0:nk, 0:nq], lhsT=kT[64:128, :], rhs=qT[64:128, q0:q0 + nq],
                                               start=True, stop=True), r=rkeys + [qk], w=[K("ps", sb + 1)])

                def p_mm(i):
                    kT, v_, nk, bias, m = ktiles[i]
                    sb = 2 * (i % 2)
                    for mp in range(2):
                        pb = Pb[(2 * i + mp) % 4]
                        pk = K("Pb", (2 * i + mp) % 4)
                        self.act(lambda e, mp=mp, pb=pb: e.activation(out=pb[0:nk, 0:nq], in_=PS[sb + mp][0:nk, 0:nq],
                                                                      func=AF.Exp, scale=0.125, bias=bias),
                                 r=[K("ps", sb + mp), K("cc")], w=[pk])
                        if m is not None:
                            self.dve(lambda e, pb=pb: e.tensor_tensor(out=pb[0:nk, 0:nq], in0=pb[0:nk, 0:nq],
                                                                      in1=maskB[0:nk, m, 0:nq], op=ALU.mult),
                                     r=[pk, K("maskB")], w=[pk])
                        ob, lb = 4 + 2 * mp, 5 + 2 * mp
                        self.pe(lambda e, pb=pb, ob=ob: e.matmul(PS[ob][:, 0:nq], lhsT=v_, rhs=pb[0:nk, 0:nq],
                                                                 start=(i == 0), stop=(i == NKTL - 1)),
                                r=rkeys + [pk], w=[K("ps", ob)])
                        self.pe(lambda e, pb=pb, lb=lb: e.matmul(PS[lb][:, 0:nq], lhsT=onesB[0:nk, :],
                                                                 rhs=pb[0:nk, 0:nq], start=(i == 0),
                                                                 stop=(i == NKTL - 1)),
                                r=[pk, K("onesB")], w=[K("ps", lb)])

                s_mm(0)
                for i in range(NKTL):
                    if i + 1 < NKTL:
                        s_mm(i + 1)
                    p_mm(i)
                    if pre_hook is not None and i == (2 if NKTL > 3 else NKTL - 1):
                        pre_hook()
                r1, r2, t1, t2, o_, sqv = [t[:, 0:nq] for t in fin]
                fk = [K("fin", i) for i in range(6)]
                self.act(lambda e: e.activation(out=r1, in_=PS[5][:, 0:nq], func=AF.Ln), r=[K("ps", 5)], w=[fk[0]])
                self.act(lambda e: e.activation(out=r2, in_=PS[7][:, 0:nq], func=AF.Ln), r=[K("ps", 7)], w=[fk[1]])
                self.act(lambda e: e.activation(out=r1, in_=r1, func=AF.Exp, scale=-1.0), r=[fk[0]], w=[fk[0]])
                self.act(lambda e: e.activation(out=r2, in_=r2, func=AF.Exp, scale=-1.0), r=[fk[1]], w=[fk[1]])
                self.dve(lambda e: e.tensor_tensor(out=t1, in0=PS[4][:, 0:nq], in1=r1, op=ALU.mult),
                         r=[K("ps", 4), fk[0]], w=[fk[2]])
                self.dve(lambda e: e.tensor_tensor(out=t2, in0=PS[6][:, 0:nq], in1=r2, op=ALU.mult),
                         r=[K("ps", 6), fk[1]], w=[fk[3]])
                self.dve(lambda e: e.scalar_tensor_tensor(out=o_, in0=t2, scalar=pv2[:, 13:14], in1=t1,
                                                          op0=ALU.mult, op1=ALU.add),
                         r=[fk[2], fk[3], K("pv2nl")], w=[fk[4]])
                self.act(lambda e: e.activation(out=sqv, in_=o_, func=AF.Square), r=[fk[4]], w=[fk[5]])

                def fin2():
                    self.pe(lambda e: e.matmul(PS[0][:, 0:nq], lhsT=onesF, rhs=sqv, start=True, stop=True),
                            r=[fk[5], K("onesF")], w=[K("ps", 0)])
                    self.act(lambda e: e.activation(out=r2, in_=PS[0][:, 0:nq], func=AF.Ln, scale=1.0 / 128,
                                                    bias=EPS), r=[K("ps", 0)], w=[fk[1]])
                    self.act(lambda e: e.activation(out=r2, in_=r2, func=AF.Exp, scale=-0.5), r=[fk[1]], w=[fk[1]])
                    self.dve(lambda e: e.scalar_tensor_tensor(out=ycat[:, 4 + h, out_col:out_col + nq], in0=o_,
                                                              scalar=pv2[:, 12:13], in1=r2, op0=ALU.mult,
                                                              op1=ALU.mult),
                             r=[fk[4], fk[1], K("pv2gs")], w=[K("ycat", "b", h)])
                return fin2

            pend = [None]
            for h in range(8):
                hp, hh = h // 2, h % 2
                sl = h % 2
                ktk, vk, qk = K("KTh", sl), K("Vh", sl), K("QTh", sl)
                self.dma("sp", QTh[sl], QT_d[h], r=[K("QT_d", h // 4)], w=[qk])
                for s_ in range(3):
                    for tc in range(NVC):
                        self.dma("sp", KTh[sl][:, s_ * TP + tc * 512:s_ * TP + (tc + 1) * 512],
                                 KT_all[tc][s_ * 1024 + h * 128:s_ * 1024 + h * 128 + 128, :],
                                 r=[K("KT_all", tc)], w=[ktk])
                    for tc in range(NVC):
                        self.dma("sp", Vh[sl][:, s_ * NKT + tc * 4:s_ * NKT + tc * 4 + 4, :],
                                 V_all[tc][s_ * 512:(s_ + 1) * 512, h * 128:(h + 1) * 128].rearrange(
                                     "(i p) d -> p i d", p=128), r=[K("V_all", tc)], w=[vk])
                for tc in range(NVC):
                    self.dma("sp", KTh[sl][:, 3 * TP + tc * 512:3 * TP + (tc + 1) * 512],
                             KT_own[tc][h * 128:h * 128 + 128, :], r=[K("KT_own", tc)], w=[ktk])
                for tc in range(NVC):
                    self.dma("sp", Vh[sl][:, 3 * NKT + tc * 4:3 * NKT + tc * 4 + 4, :],
                             V_own[tc][:, h * 128:(h + 1) * 128].rearrange("(i p) d -> p i d", p=128),
                             r=[K("V_own", tc)], w=[vk])
                self.dma("pool", kc, ck[l, :, h * 128:(h + 1) * 128].rearrange("(kt p) e -> p kt e", p=128),
                         w=[K("kc")])
                self.dma("pool", vc[:, 0:PKT, :], cv[l, :, h * 128:(h + 1) * 128].rearrange("(kt p) e -> p kt e", p=128),
                         w=[K("vc")])
                self.dve(lambda e, h=h: e.tensor_copy(out=vc[0:TS, PKT, :], in_=vsB[0:TS, h * 128:(h + 1) * 128]),
                         r=[K("vsB"), K("vc")], w=[K("vc")])
                for qb in range(NCH):
                    kts = []
                    for s_ in range(3):
                        for kt in range(NKT):
                            c_ = s_ * TP + kt * 128
                            kts.append((KTh[sl][:, c_:c_ + 128], Vh[sl][:, s_ * NKT + kt, :], 128, cc[:, s_:s_ + 1],
                                        None))
                    for kt in range(4 * qb + 4):
                        c_ = 3 * TP + kt * 128
                        m = kt - 4 * qb
                        kts.append((KTh[sl][:, c_:c_ + 128], Vh[sl][:, 3 * NKT + kt, :], 128, cc[:, 11:12],
                                    m if m >= 0 else None))
                    pend[0] = attend(h, QTh[sl], qk, qb * 512, 512, kts, qb * 512, [ktk, vk], pre_hook=pend[0])
                for g4 in range(PKT // 4):
                    bank = g4 % 4
                    for i in range(4):
                        kt = g4 * 4 + i
                        self.pe(lambda e, kt=kt, i=i, bank=bank: e.transpose(
                            out=psB(bank)[:, i * 128:(i + 1) * 128], in_=kc[:, kt, :], identity=identB),
                            r=[K("kc"), K("identB")], w=[K("ps", bank)])
                    self.act(lambda e, g4=g4, bank=bank: e.activation(out=KTc[:, g4 * 512:(g4 + 1) * 512],
                                                                     in_=psB(bank)[:, 0:512], func=AF.Copy),
                             r=[K("ps", bank)], w=[K("KTc")])
                self.dve(lambda e, h=h: e.tensor_copy(out=KTc[:, PAST:PAST + TS], in_=ksT[:, h, :]),
                         r=[K("ksT"), K("KTc")], w=[K("KTc")])
                kts = []
                for kt in range(PKT):
                    kts.append((KTc[:, kt * 128:(kt + 1) * 128], vc[:, kt, :], 128, cc[:, 11:12], None))
                kts.append((KTc[:, PAST:PAST + TS], vc[0:TS, PKT, :], TS, cc[0:TS, 11:12], None))
                pend[0] = attend(h, QTh[sl], qk, TP, TS, kts, TP, [K("KTc"), K("vc")], pre_hook=pend[0])

            pend[0]()
            fence()
            if dbg_ycat is not None and l == 0:
                self.dma("sp", dbg_ycat[:, :, :], ycat, r=[K("ycat", "c")], w=[K("dbg")])
                fence()
            for cb in range(4):
                self.dma("pool", wout_sb[:, :, cb * 512:(cb + 1) * 512],
                         w_out[l, :, cb * 512:(cb + 1) * 512].rearrange("(kt p) c -> p kt c", p=128),
                         w=[K("wout", cb)])
            self.dma("sp", gpostA, bvec[l:l + 1, 0:2048].partition_broadcast(128), w=[K("gpostA")])
            ycat_keys = [K("ycat", "a", c) for c in range(4)] + [K("ycat", "b", h) for h in range(8)] + [K("ycat", "c")]
            for tt in range(NT):
                n = tokn(tt)
                c0 = tcol(tt)
                xtile = wxt[tt % 2]
                xk = K("wxt", tt % 2)
                self.dma("sp", xtile[0:n, :], x_src(l, tt), r=[xkey(tt)], w=[xk])
                for cb in range(4):
                    bank = cb + 4 * (tt % 2)
                    for ct in range(16):
                        self.pe(lambda e, ct=ct, cb=cb, bank=bank: e.matmul(
                            PS[bank][0:n, :], lhsT=ycat[:, ct, c0:c0 + n], rhs=wout_sb[:, ct, cb * 512:(cb + 1) * 512],
                            start=(ct == 0), stop=(ct == 15)), r=ycat_keys + [K("wout", cb)], w=[K("ps", bank)])
                    self.act(lambda e, bank=bank, cb=cb: e.activation(out=wjunk[0:n, :], in_=PS[bank][0:n, :],
                                                                    func=AF.Square,
                                                                    accum_out=small[0:n, 40 + cb:41 + cb]),
                             r=[K("ps", bank)], w=[K("wjunk"), K("wss", cb)])
                self.dve(lambda e: e.tensor_reduce(out=small[0:n, 44:45], in_=small[0:n, 40:44], axis=X, op=ALU.add),
                         r=[K("wss", c) for c in range(4)], w=[K("wss4")])
                rstd_from_ss(small[0:n, 44:45], small[0:n, 45:46], D, [K("wss4")], [K("wrstd")])
                for cb in range(4):
                    bank = cb + 4 * (tt % 2)
                    tmp = wtmp[cb]
                    self.dve(lambda e, bank=bank, cb=cb, tmp=tmp: e.scalar_tensor_tensor(
                        out=tmp[0:n, :], in0=PS[bank][0:n, :], scalar=small[0:n, 45:46],
                        in1=gpostA[0:n, cb * 512:(cb + 1) * 512], op0=ALU.mult, op1=ALU.mult),
                        r=[K("ps", bank), K("wrstd"), K("gpostA")], w=[K("wtmp", cb)])
                    self.dve(lambda e, cb=cb, tmp=tmp: e.tensor_tensor(
                        out=xtile[0:n, cb * 512:(cb + 1) * 512], in0=xtile[0:n, cb * 512:(cb + 1) * 512],
                        in1=tmp[0:n, :], op=ALU.add), r=[K("wtmp", cb), xk], w=[xk])
                self.dma("sp", xres_ap(tt), xtile[0:n, :], r=[xk], w=[xkey(tt)])

            fence()
            self.dma("sp", gpostF, bvec[l:l + 1, 2048:4096].partition_broadcast(128), w=[K("gpostF")])
            gu_n = [0]
            wd_n = [0]
            for ci in range(NCH):
                tiles = chunk_tiles(ci)
                subs = chunk_subs(ci)
                for (tt, lc, n) in tiles:
                    xtile = fxt[tt % 2]
                    xk = K("fxt", tt % 2)
                    self.dma("sp", xtile[0:n, :], xres_ap(tt), r=[xkey(tt)], w=[xk])
                    norm_transpose(n, xtile, xk, hnT, lc, K("hnT"), 16, fxnb, "F")
                for fb in range(FT // 2):
                    sg = gub[gu_n[0] % 4]
                    kg = K("gub", gu_n[0] % 4)
                    gu_n[0] += 1
                    su = gub[gu_n[0] % 4]
                    ku = K("gub", gu_n[0] % 4)
                    gu_n[0] += 1
                    self.dma("pool", sg, w_gate[l, :, fb * 256:(fb + 1) * 256].rearrange("(kt p) c -> p kt c", p=128),
                             w=[kg])
                    self.dma("pool", su, w_up[l, :, fb * 256:(fb + 1) * 256].rearrange("(kt p) c -> p kt c", p=128),
                             w=[ku])
                    for fi in range(2):
                        f = 2 * fb + fi
                        for (c0, ncol) in subs:
                            bg = bankrr[0] % 8
                            bu = (bankrr[0] + 1) % 8
                            bankrr[0] += 2
                            for dt_ in range(DT):
                                self.pe(lambda e, dt_=dt_, bg=bg: e.matmul(
                                    PS[bg][:, 0:ncol], lhsT=sg[:, dt_, fi * 128:(fi + 1) * 128],
                                    rhs=hnT[:, dt_, c0:c0 + ncol], start=(dt_ == 0), stop=(dt_ == DT - 1)),
                                    r=[kg, K("hnT")], w=[K("ps", bg)])
                            for dt_ in range(DT):
                                self.pe(lambda e, dt_=dt_, bu=bu: e.matmul(
                                    PS[bu][:, 0:ncol], lhsT=su[:, dt_, fi * 128:(fi + 1) * 128],
                                    rhs=hnT[:, dt_, c0:c0 + ncol], start=(dt_ == 0), stop=(dt_ == DT - 1)),
                                    r=[ku, K("hnT")], w=[K("ps", bu)])
                            ti_ = (bankrr[0] // 2) % 2
                            e_t = ft_[2 * ti_][:, 0:ncol]
                            t_t = ft_[2 * ti_ + 1][:, 0:ncol]
                            ek, tk = K("ft", 2 * ti_), K("ft", 2 * ti_ + 1)
                            self.act(lambda e, bg=bg, e_t=e_t: e.activation(out=e_t, in_=PS[bg][:, 0:ncol],
                                                                            func=AF.Exp, scale=-1.0),
                                     r=[K("ps", bg)], w=[ek])
                            self.act(lambda e, e_t=e_t: e.activation(out=e_t, in_=e_t, func=AF.Ln, scale=1.0,
                                                                     bias=1.0), r=[ek], w=[ek])
                            self.act(lambda e, e_t=e_t: e.activation(out=e_t, in_=e_t, func=AF.Exp, scale=-1.0),
                                     r=[ek], w=[ek])
                            self.dve(lambda e, bg=bg, e_t=e_t, t_t=t_t: e.tensor_tensor(
                                out=t_t, in0=PS[bg][:, 0:ncol], in1=e_t, op=ALU.mult), r=[K("ps", bg), ek], w=[tk])
                            self.dve(lambda e, bu=bu, t_t=t_t, f=f: e.tensor_tensor(
                                out=ffT[:, f, c0:c0 + ncol], in0=PS[bu][:, 0:ncol], in1=t_t, op=ALU.mult),
                                r=[K("ps", bu), tk], w=[K("ffT")])
                for cb in range(4):
                    for q in range(4):
                        sw = wdb[wd_n[0] % 2]
                        kw = K("wdb", wd_n[0] % 2)
                        wd_n[0] += 1
                        self.dma("pool", sw, w_down[l, q * FQ * 128:(q + 1) * FQ * 128,
                                                    cb * 512:(cb + 1) * 512].rearrange("(f p) c -> p f c", p=128),
                                 w=[kw])
                        for ti, (tt, lc, n) in enumerate(tiles):
                            for f in range(FQ):
                                self.pe(lambda e, f=f, ti=ti, lc=lc, n=n, q=q: e.matmul(
                                    PS[ti][0:n, :], lhsT=ffT[:, q * FQ + f, lc:lc + n], rhs=sw[:, f, :],
                                    start=(q == 0 and f == 0), stop=(q == 3 and f == FQ - 1)),
                                    r=[K("ffT"), kw], w=[K("ps", ti)])
                    for ti, (tt, lc, n) in enumerate(tiles):
                        ys = ystg[(cb * 5 + ti) % 2]
                        yk = K("ystg", (cb * 5 + ti) % 2)
                        self.act(lambda e, ti=ti, n=n, ys=ys: e.activation(out=ys[0:n, :], in_=PS[ti][0:n, :],
                                                                          func=AF.Copy), r=[K("ps", ti)], w=[yk])
                        self.act(lambda e, ti=ti, n=n, cb=cb: e.activation(
                            out=fjunk[0:n, :], in_=PS[ti][0:n, :], func=AF.Square,
                            accum_out=small[0:n, 48 + ti * 4 + cb:49 + ti * 4 + cb]),
                            r=[K("ps", ti)], w=[K("fjunk"), K("fss", ti, cb)])
                        yr0 = 128 * tt if tt < NPT else TP
                        self.dma("sp", yscr[yr0:yr0 + n, cb * 512:(cb + 1) * 512], ys[0:n, :], r=[yk],
                                 w=[K("yscr", tt)])
                for ti, (tt, lc, n) in enumerate(tiles):
                    xtile = fxt[tt % 2]
                    xk = K("fxt", tt % 2)
                    ytile = fyt[tt % 2]
                    yk = K("fyt", tt % 2)
                    yr0 = 128 * tt if tt < NPT else TP
                    self.dma("sp", xtile[0:n, :], xres_ap(tt), r=[xkey(tt)], w=[xk])
                    self.dma("sp", ytile[0:n, :], yscr[yr0:yr0 + n, :], r=[K("yscr", tt)], w=[yk])
                    self.dve(lambda e, ti=ti, n=n: e.tensor_reduce(out=small[0:n, 70:71],
                                                                  in_=small[0:n, 48 + ti * 4:52 + ti * 4], axis=X,
                                                                  op=ALU.add),
                             r=[K("fss", ti, c) for c in range(4)], w=[K("fss4")])
                    rstd_from_ss(small[0:n, 70:71], small[0:n, 71:72], D, [K("fss4")], [K("frstd")])
                    self.dve(lambda e, n=n, ytile=ytile: e.scalar_tensor_tensor(
                        out=ytile[0:n, :], in0=ytile[0:n, :], scalar=small[0:n, 71:72], in1=gpostF[0:n, :],
                        op0=ALU.mult, op1=ALU.mult), r=[yk, K("frstd"), K("gpostF")], w=[yk])
                    self.dve(lambda e, n=n, ytile=ytile, xtile=xtile: e.tensor_tensor(
                        out=xtile[0:n, :], in0=xtile[0:n, :], in1=ytile[0:n, :], op=ALU.add), r=[yk, xk], w=[xk])
                    if final:
                        self.dma("sp", yout_ap(tt), xtile[0:n, :], r=[xk], w=[K("yout", tt)])
                    else:
                        self.dma("sp", xres_ap(tt), xtile[0:n, :], r=[xk], w=[xkey(tt)])
            fence()

        S.emit(nc)
        return nc


def _host_consts(cfg, j):
    NT, NPT, TP, PAST = cfg.NT, cfg.NPT, cfg.TP, cfg.PAST
    half = 8
    inv_freq = np.power(np.float32(500000.0), -np.arange(half, dtype=np.float32) * np.float32(2.0 / 16)).astype(np.float32)
    cosd = np.zeros((128, NT, 64), np.float32)
    sind = np.zeros((128, NT, 64), np.float32)
    for tt in range(NT):
        if tt < NPT:
            pos = (j * TP + 128 * tt + np.arange(128)).astype(np.float32)
        else:
            pos = np.zeros(128, np.float32)
            pos[:TS] = (PAST + np.arange(TS)).astype(np.float32)
        ang = pos[:, None] * inv_freq[None, :]
        cosd[:, tt, :] = np.tile(np.cos(ang).astype(np.float32), (1, 8))
        sind[:, tt, :] = np.tile(np.sin(ang).astype(np.float32), (1, 8))
    cc = np.zeros((128, 12), np.float32)
    for s in range(3):
        cc[:, s] = 0.0 if s < j else NEG
    for r in range(4):
        cc[:, 3 + r] = 1.0 if r == j - 1 else 0.0
        cc[:, 7 + r] = 1.0 if r == j else 0.0
    masks = np.zeros((128, 4, 512), np.float32)
    k = np.arange(128)[:, None]
    q = np.arange(512)[None, :]
    for m in range(4):
        masks[:, m, :] = (((128 * m + k) // 64) <= (q // 64)).astype(np.float32)
    triu = np.triu(np.ones((128, 128), np.float32))
    ident = np.eye(128, dtype=np.float32)
    return cosd, sind, cc, masks, triu, ident


def _fm(v):
    n = v.shape[-1] // 128
    return np.swapaxes(v.reshape(v.shape[:-1] + (n, 128)), -1, -2)


def run(cfg, inputs, trace=False):
    L, TP, PAST = cfg.L, cfg.TP, cfg.PAST
    f32 = np.float32
    A = lambda k: np.ascontiguousarray(np.asarray(inputs[k], dtype=f32))
    x_prompt, x_sample = A("x_prompt"), A("x_sample")
    cache_k, cache_v = A("cache_k"), A("cache_v")
    st_h, st_c = A("state_lru_h"), A("state_conv")
    w_rg = np.ascontiguousarray(np.concatenate([A("w_rg_a"), A("w_rg_x")], axis=1))
    bvec = np.ascontiguousarray(np.concatenate([
        A("g_mix_post"), A("g_ffn_post"), A("g_mlp_v"), A("b_mlp_v"), A("b_spatial").reshape(L, 512),
        A("lam_q1"), A("lam_k1"), A("lam_q2"), A("lam_k2")], axis=1))
    common_pv = [
        _fm(A("g_mix_pre")), _fm(A("g_ffn_pre")),
        np.swapaxes(_fm(A("conv_w")), 1, 2).reshape(L, 128, 4, 4).transpose(0, 1, 3, 2).reshape(L, 128, 16),
        _fm(A("conv_b")), _fm(A("b_rg_a")), _fm(A("b_rg_x")), _fm(A("lru_lambda")),
        A("g_subln").reshape(L, 128, 1),
    ]
    common_pv[2] = _fm(A("conv_w")).transpose(0, 2, 3, 1).reshape(L, 128, 16)
    shared = {
        "w_in": A("w_in"), "w_out": A("w_out"), "w_gate": A("w_gate"), "w_up": A("w_up"), "w_down": A("w_down"),
        "w_rg": w_rg, "w_sp": A("w_spatial"), "bvec": bvec,
    }
    in_maps = []
    for c in range(8):
        b, j = c // 4, c % 4
        cosd, sind, cc, masks, triu, ident = _host_consts(cfg, j)
        pv = np.concatenate(common_pv + [
            _fm(st_h[:, c]),
            _fm(st_c[:, c]).transpose(0, 2, 3, 1).reshape(L, 128, 12),
        ], axis=2)
        m = dict(shared)
        m.update({
            "xp": np.ascontiguousarray(x_prompt[b, j * TP:(j + 1) * TP]),
            "xs": np.ascontiguousarray(x_sample[c]),
            "ck": np.ascontiguousarray(cache_k[:, c].reshape(L, PAST, 1024)),
            "cv": np.ascontiguousarray(cache_v[:, c].reshape(L, PAST, 1024)),
            "pvec": np.ascontiguousarray(pv.astype(f32)),
            "cosd": cosd, "sind": sind, "cconst": cc, "masks": masks, "triu": triu, "ident": ident,
        })
        in_maps.append(m)
    bld = Builder(cfg)
    nc = bld.build()
    res = run_bass_kernel_spmd(nc, in_maps, core_ids=list(range(8)), **({"trace": True} if trace else {}))
    R = res.results
    G = lambda c, k: np.asarray(R[c][k], dtype=f32)
    SEQ = 4 * TP
    y_prompt = np.stack([np.concatenate([G(b * 4 + j, "y_p") for j in range(4)], 0) for b in range(2)], 0)
    y_sample = np.stack([G(c, "y_s") for c in range(8)], 0)
    k_prompt = np.stack([np.concatenate([G(b * 4 + j, "k_p") for j in range(4)], 1) for b in range(2)], 1)
    v_prompt = np.stack([np.concatenate([G(b * 4 + j, "v_p") for j in range(4)], 1) for b in range(2)], 1)
    k_prompt = k_prompt.reshape(L, 2, SEQ, 8, 128)
    v_prompt = v_prompt.reshape(L, 2, SEQ, 8, 128)
    unfm = lambda a: np.swapaxes(a, -1, -2).reshape(a.shape[:-2] + (512,))
    h_prompt = np.stack([unfm(G(b * 4 + 3, "h_p")) for b in range(2)], 1)
    uc = lambda a: a.transpose(0, 3, 2, 1).reshape(L, 3, 512)
    conv_prompt = np.stack([uc(G(b * 4 + 3, "c_p")) for b in range(2)], 1)
    k_sample = np.stack([G(c, "k_s") for c in range(8)], 1).reshape(L, 8, TS, 8, 128)
    v_sample = np.stack([G(c, "v_s") for c in range(8)], 1).reshape(L, 8, TS, 8, 128)
    h_sample = np.stack([unfm(G(c, "h_s")) for c in range(8)], 1)
    conv_sample = np.stack([uc(G(c, "c_s")) for c in range(8)], 1)
    chunk_v = np.stack([G(c, "cv_s") for c in range(8)], 1)
    outs = (y_prompt, y_sample, k_prompt, v_prompt, h_prompt, conv_prompt, k_sample, v_sample, h_sample,
            conv_sample, chunk_v)
    return tuple(np.ascontiguousarray(o.astype(f32)) for o in outs), res


def kernel(**inputs):
    cfg = Cfg()
    outs, _ = run(cfg, inputs)
    return outs
```
